# Optimizing a Trainium2 kernel written in Bass

```python
import math
import jax, jax.numpy as jnp
from jax import lax
import numpy as np

D_MODEL = 1024
BATCH = 2
SEQ = 8192
DEPTH = 2

GRID_W = 64
CTX_LEN = 256
D_FF = 2816
N_MOD = 9
LN_EPS = 1e-5
S5_WIDTH = 512
S5_GROUP = 16
S5_GROUPS = S5_WIDTH // S5_GROUP
S5_STATE = 64
S5_DT_MIN = 0.001
S5_DT_MAX = 0.1
NA_HEADS = 8
NA_HEAD_DIM = 64
NA_WIDTH = NA_HEADS * NA_HEAD_DIM
WIN_H = 8
WIN_W = 16
CONV_WIDTH = 512
CONV_K = 3
N_BRANCH = 3
BRANCH_WIDTH = 512
COL_U = 0
COL_K = COL_U + S5_WIDTH
COL_V = COL_K + NA_WIDTH
COL_Q = COL_V + NA_WIDTH
COL_Z = COL_Q + NA_WIDTH
COL_B = COL_Z + CONV_WIDTH
COL_C = COL_B + CONV_WIDTH
COL_G = COL_C + CONV_WIDTH
D_IN = COL_G + N_BRANCH * D_MODEL

kernel_name = 'hybrid_s5_natten_shortconv_macaron_trunk'


def layer_norm(x, g, b):
    xf = x.astype(jnp.float32)
    mu = jnp.mean(xf, -1, keepdims=True)
    var = jnp.mean(jnp.square(xf - mu), -1, keepdims=True)
    y = (xf - mu) * lax.rsqrt(var + LN_EPS) * g.astype(jnp.float32) + b.astype(jnp.float32)
    return y.astype(x.dtype)


def modulate(x, shift, scale):
    return x * (1.0 + scale) + shift


def swiglu(h, wg, wu, wd):
    return (jax.nn.silu(h @ wg) * (h @ wu)) @ wd


def ffn_sublayer(x, shift, scale, gate, wg, wu, wd, g, b, alpha):
    h = modulate(x, shift, scale)
    return layer_norm(alpha * x + 0.5 * gate * swiglu(h, wg, wu, wd), g, b)


def split_heads(t):
    return t.reshape(t.shape[:-1] + (NA_HEADS, NA_HEAD_DIM))


def s5_discretise(lam_re, lam_im, log_dt, b_re, b_im, c_re, c_im):
    f32 = jnp.float32
    lam = lax.complex(lam_re.astype(f32), lam_im.astype(f32))
    dt = jnp.exp(log_dt.astype(f32))[..., None]
    lam_bar = jnp.exp(lam * dt)
    b = lax.complex(b_re.astype(f32), b_im.astype(f32))
    b_bar = ((lam_bar - 1.0) / lam)[..., None] * b
    cc = lax.complex(c_re.astype(f32), c_im.astype(f32))
    return lam_bar, b_bar, cc


def _recurrence_combine(e1, e2):
    a1, b1 = e1
    a2, b2 = e2
    return a1 * a2, a2 * b1 + b2


def s5_scan(ug, lam_bar, b_bar, s0, reverse):
    t_len = ug.shape[1]
    if reverse:
        ug = jnp.flip(ug, 1)
    bu = jnp.einsum('gph,btgh->btgp', b_bar, ug.astype(jnp.float32).astype(jnp.complex64))
    bu = bu.at[:, 0].add(lam_bar * s0)
    a = jnp.broadcast_to(lam_bar, (1, t_len) + lam_bar.shape)
    _, states = lax.associative_scan(_recurrence_combine, (a, bu), axis=1)
    if reverse:
        states = jnp.flip(states, 1)
    return states


def s5_states(u, lam_bar, b_bar, s0_fwd, s0_bwd):
    bsz, t_len, _ = u.shape
    ug = u.reshape(bsz, t_len, S5_GROUPS, S5_GROUP)
    s_fwd = s5_scan(ug, lam_bar[0], b_bar[0], s0_fwd, False)
    s_bwd = s5_scan(ug, lam_bar[1], b_bar[1], s0_bwd, True)
    return s_fwd, s_bwd


def s5_readout(u, s_fwd, s_bwd, cc, d, w_glu):
    bsz, t_len, _ = u.shape
    y = jnp.real(jnp.einsum('ghp,btgp->btgh', cc[0], s_fwd) + jnp.einsum('ghp,btgp->btgh', cc[1], s_bwd))
    y = y.reshape(bsz, t_len, S5_WIDTH).astype(u.dtype) + d * u
    g = jax.nn.gelu(y)
    return g * jax.nn.sigmoid(g @ w_glu)


def neighbourhood_attention(q, k, v, k_ctx, v_ctx, rpb):
    bsz, s_len, n_h, dh = q.shape
    rows = s_len // GRID_W
    kh = min(WIN_H, rows)
    r = jnp.arange(rows)
    row_start = jnp.clip(r - WIN_H // 2, 0, rows - kh)
    row_idx = row_start[:, None] + jnp.arange(kh)[None, :]
    col = jnp.arange(GRID_W)
    col_start = jnp.clip(col - WIN_W // 2, 0, GRID_W - WIN_W)
    col_mask = (col[None, :] >= col_start[:, None]) & (col[None, :] < col_start[:, None] + WIN_W)
    dr = row_idx - r[:, None] + WIN_H - 1
    dc = jnp.clip(col[None, :] - col[:, None] + WIN_W - 1, 0, 2 * WIN_W - 2)
    bias = rpb.astype(jnp.float32)[:, dr]
    bias = jnp.transpose(bias[..., dc], (0, 1, 3, 2, 4))
    scale = dh ** -0.5
    qg = q.reshape(bsz, rows, GRID_W, n_h, dh)
    kg = k.reshape(bsz, rows, GRID_W, n_h, dh)[:, row_idx]
    vg = v.reshape(bsz, rows, GRID_W, n_h, dh)[:, row_idx].reshape(bsz, rows, kh * GRID_W, n_h, dh)
    s_win = jnp.einsum('brchd,brjkhd->bhrcjk', qg, kg).astype(jnp.float32) * scale + bias[None]
    s_win = jnp.where(col_mask[:, None, :], s_win, -jnp.inf).reshape(bsz, n_h, rows, GRID_W, kh * GRID_W)
    s_ctx = jnp.einsum('brchd,blhd->bhrcl', qg, k_ctx).astype(jnp.float32) * scale
    p = jax.nn.softmax(jnp.concatenate([s_win, s_ctx], axis=-1), axis=-1).astype(v.dtype)
    n_win = kh * GRID_W
    out = (jnp.einsum('bhrcn,brnhd->brchd', p[..., :n_win], vg)
           + jnp.einsum('bhrcl,blhd->brchd', p[..., n_win:], v_ctx))
    return out.reshape(bsz, s_len, n_h * dh)


def context_attention(q, k, v):
    bsz, l_len, n_h, dh = q.shape
    s = jnp.einsum('blhd,bmhd->bhlm', q, k).astype(jnp.float32) * dh ** -0.5
    p = jax.nn.softmax(s, axis=-1).astype(v.dtype)
    return jnp.einsum('bhlm,bmhd->blhd', p, v).reshape(bsz, l_len, n_h * dh)


def short_conv(z, bg, cg, conv_w):
    vv = cg * z
    t_len = vv.shape[1]
    pad = CONV_K // 2
    vp = jnp.pad(vv, ((0, 0), (pad, pad), (0, 0)))
    y = conv_w[0] * vp[:, 0:t_len]
    for i in range(1, CONV_K):
        y = y + conv_w[i] * vp[:, i:i + t_len]
    return bg * y


def merge_branches(gate_logits, y_a, y_b, y_c, w_branch, w_out):
    g = jax.nn.sigmoid(gate_logits).reshape(gate_logits.shape[:-1] + (N_BRANCH, D_MODEL))
    m = (g[..., 0, :] * (y_a @ w_branch[0]) + g[..., 1, :] * (y_b @ w_branch[1])
         + g[..., 2, :] * (y_c @ w_branch[2]))
    return m @ w_out


def setup_inputs(seed: int = 0) -> dict:
    key = jax.random.key(seed)
    ks = jax.random.split(key, 26)
    f32 = jnp.float32
    beta = (8.0 * DEPTH) ** -0.25

    def nrm(k, shape, s):
        return jax.random.normal(k, shape, f32) * s

    G, P, Hg = S5_GROUPS, S5_STATE, S5_GROUP
    n = jnp.arange(S5_STATE, dtype=f32)
    return {
        'x': nrm(ks[0], (BATCH, SEQ, D_MODEL), 1.0),
        'c': nrm(ks[1], (BATCH, D_MODEL), 1.0),
        'ctx': nrm(ks[2], (BATCH, CTX_LEN, D_MODEL), 1.0),
        'c_ctx': nrm(ks[3], (D_MODEL,), 1.0),
        'w_mod': nrm(ks[4], (DEPTH, D_MODEL, N_MOD * D_MODEL), 0.5 * D_MODEL ** -0.5),
        'b_mod': nrm(ks[5], (DEPTH, N_MOD * D_MODEL), 0.01),
        'ln_g': 1.0 + nrm(ks[6], (DEPTH, 3, D_MODEL), 0.02),
        'ln_b': nrm(ks[7], (DEPTH, 3, D_MODEL), 0.02),
        'ffn_wg': nrm(ks[8], (DEPTH, 2, D_MODEL, D_FF), D_MODEL ** -0.5),
        'ffn_wu': nrm(ks[9], (DEPTH, 2, D_MODEL, D_FF), D_MODEL ** -0.5),
        'ffn_wd': nrm(ks[10], (DEPTH, 2, D_FF, D_MODEL), beta * D_FF ** -0.5),
        'w_in': nrm(ks[11], (DEPTH, D_MODEL, D_IN), D_MODEL ** -0.5),
        's5_lam_re': -0.5 + nrm(ks[12], (DEPTH, 2, G, P), 0.01),
        's5_lam_im': math.pi * n + nrm(ks[13], (DEPTH, 2, G, P), 0.01),
        's5_log_dt': jax.random.uniform(ks[14], (DEPTH, 2, G), f32, math.log(S5_DT_MIN), math.log(S5_DT_MAX)),
        's5_b_re': nrm(ks[15], (DEPTH, 2, G, P, Hg), (2.0 * Hg) ** -0.5),
        's5_b_im': nrm(ks[16], (DEPTH, 2, G, P, Hg), (2.0 * Hg) ** -0.5),
        's5_c_re': nrm(ks[17], (DEPTH, 2, G, Hg, P), (2.0 * P) ** -0.5),
        's5_c_im': nrm(ks[18], (DEPTH, 2, G, Hg, P), (2.0 * P) ** -0.5),
        's5_d': nrm(ks[19], (DEPTH, S5_WIDTH), 1.0),
        's5_w_glu': nrm(ks[20], (DEPTH, S5_WIDTH, S5_WIDTH), S5_WIDTH ** -0.5),
        'na_rpb': nrm(ks[21], (DEPTH, NA_HEADS, 2 * WIN_H - 1, 2 * WIN_W - 1), 0.1),
        'conv_w': nrm(ks[22], (DEPTH, CONV_K, CONV_WIDTH), CONV_K ** -0.5),
        'w_branch': nrm(ks[23], (DEPTH, N_BRANCH, BRANCH_WIDTH, D_MODEL), BRANCH_WIDTH ** -0.5),
        'w_out': nrm(ks[24], (DEPTH, D_MODEL, D_MODEL), beta * D_MODEL ** -0.5),
    }


def reference(x, c, ctx, c_ctx, w_mod, b_mod, ln_g, ln_b, ffn_wg, ffn_wu, ffn_wd, w_in,
              s5_lam_re, s5_lam_im, s5_log_dt, s5_b_re, s5_b_im, s5_c_re, s5_c_im, s5_d, s5_w_glu,
              na_rpb, conv_w, w_branch, w_out):
    bsz = x.shape[0]
    alpha = (2.0 * DEPTH) ** 0.25
    silu_c = jax.nn.silu(c)[:, None, :]
    silu_cc = jax.nn.silu(c_ctx)
    xc = ctx
    for l in range(DEPTH):
        last = l == DEPTH - 1
        mod = (silu_c @ w_mod[l] + b_mod[l]).reshape(bsz, 1, N_MOD, D_MODEL)
        mod = [mod[:, :, i] for i in range(N_MOD)]
        modc = (silu_cc @ w_mod[l] + b_mod[l]).reshape(N_MOD, D_MODEL)

        x = ffn_sublayer(x, mod[0], mod[1], mod[2], ffn_wg[l, 0], ffn_wu[l, 0], ffn_wd[l, 0],
                         ln_g[l, 0], ln_b[l, 0], alpha)
        xc = ffn_sublayer(xc, modc[0], modc[1], modc[2], ffn_wg[l, 0], ffn_wu[l, 0], ffn_wd[l, 0],
                          ln_g[l, 0], ln_b[l, 0], alpha)

        lam_bar, b_bar, cc = s5_discretise(s5_lam_re[l], s5_lam_im[l], s5_log_dt[l],
                                           s5_b_re[l], s5_b_im[l], s5_c_re[l], s5_c_im[l])
        h = modulate(x, mod[3], mod[4])
        hc = modulate(xc, modc[3], modc[4])

        proj_c = hc @ (w_in[l][:, :COL_Q] if last else w_in[l])
        u_c = proj_c[..., COL_U:COL_K]
        k_c = split_heads(proj_c[..., COL_K:COL_V])
        v_c = split_heads(proj_c[..., COL_V:COL_Q])
        zero_state = jnp.zeros((bsz, S5_GROUPS, S5_STATE), jnp.complex64)
        sf_c, sb_c = s5_states(u_c, lam_bar, b_bar, zero_state, zero_state)

        proj = h @ w_in[l]
        u = proj[..., COL_U:COL_K]
        sf, sb = s5_states(u, lam_bar, b_bar, sf_c[:, -1], sb_c[:, 0])
        y_a = s5_readout(u, sf, sb, cc, s5_d[l], s5_w_glu[l])
        y_b = neighbourhood_attention(split_heads(proj[..., COL_Q:COL_Z]), split_heads(proj[..., COL_K:COL_V]),
                                      split_heads(proj[..., COL_V:COL_Q]), k_c, v_c, na_rpb[l])
        y_c = short_conv(proj[..., COL_Z:COL_B], proj[..., COL_B:COL_C], proj[..., COL_C:COL_G], conv_w[l])
        mix = merge_branches(proj[..., COL_G:], y_a, y_b, y_c, w_branch[l], w_out[l])
        x = layer_norm(alpha * x + mod[5] * mix, ln_g[l, 1], ln_b[l, 1])

        x = ffn_sublayer(x, mod[6], mod[7], mod[8], ffn_wg[l, 1], ffn_wu[l, 1], ffn_wd[l, 1],
                         ln_g[l, 2], ln_b[l, 2], alpha)

        if not last:
            yc_a = s5_readout(u_c, sf_c, sb_c, cc, s5_d[l], s5_w_glu[l])
            yc_b = context_attention(split_heads(proj_c[..., COL_Q:COL_Z]), k_c, v_c)
            yc_c = short_conv(proj_c[..., COL_Z:COL_B], proj_c[..., COL_B:COL_C], proj_c[..., COL_C:COL_G], conv_w[l])
            mix_c = merge_branches(proj_c[..., COL_G:], yc_a, yc_b, yc_c, w_branch[l], w_out[l])
            xc = layer_norm(alpha * xc + modc[5] * mix_c, ln_g[l, 1], ln_b[l, 1])
            xc = ffn_sublayer(xc, modc[6], modc[7], modc[8], ffn_wg[l, 1], ffn_wu[l, 1], ffn_wd[l, 1],
                              ln_g[l, 2], ln_b[l, 2], alpha)
    return x
```

```python
import contextlib
from concourse.bass_utils import run_bass_kernel_spmd
import numpy as np
import concourse.bass as bass
import concourse.mybir as mybir

F32 = mybir.dt.float32
BF16 = mybir.dt.bfloat16
ALU = mybir.AluOpType
AF = mybir.ActivationFunctionType
AX = mybir.AxisListType

ENGS = ["pe", "act", "dve", "pool", "sp"]
NDMA = 12


class Op:
    __slots__ = ("eng", "fn", "waits", "sem", "inc", "final")

    def __init__(self, eng, fn):
        self.eng = eng
        self.fn = fn
        self.waits = []
        self.sem = None
        self.inc = 0
        self.final = False


class Prog:
    def __init__(self, nc):
        self.nc = nc
        self.ops = []
        self.stack = contextlib.ExitStack()
        self.cnt = {e: 0 for e in ENGS}
        self.known = {e: {} for e in ENGS}
        self.lastw = {}
        self.readers = {}
        self.semh = {}
        for e in ["pe", "act", "dve", "pool"]:
            self.semh[e] = self.stack.enter_context(nc.semaphore("c_" + e))
        self.dma_tot = {}
        self.dma_rr = {}
        for q in ["sp", "pool", "act"]:
            for i in range(NDMA):
                k = ("dma", q, i)
                self.semh[k] = self.stack.enter_context(nc.semaphore("d_%s%d" % (q, i)))
                self.dma_tot[k] = 0
            self.dma_rr[q] = 0
        self.out_dmas = []

    def sb(self, name, shape, dt):
        return self.stack.enter_context(self.nc.sbuf_tensor(name, list(shape), dt))

    def ps(self, name, shape, dt):
        return self.stack.enter_context(self.nc.psum_tensor(name, list(shape), dt))

    def _need(self, op, eng, semkey, val):
        if semkey == "pe" and eng == "pe":
            return
        if self.known[eng].get(semkey, 0) >= val:
            return
        self.known[eng][semkey] = val
        op.waits.append((semkey, val))

    def op(self, eng, fn, r=(), w=(), dma=False, is_out=False):
        o = Op(eng, fn)
        for k in r:
            lw = self.lastw.get(k)
            if lw is not None:
                self._need(o, eng, lw[0], lw[1])
        for k in w:
            lw = self.lastw.get(k)
            if lw is not None:
                self._need(o, eng, lw[0], lw[1])
            for rd in self.readers.get(k, ()):
                self._need(o, eng, rd[0], rd[1])
        if dma:
            q = eng
            i = self.dma_rr[q]
            self.dma_rr[q] = (i + 1) % NDMA
            sk = ("dma", q, i)
            if self.dma_tot[sk] > 0:
                self._need(o, eng, sk, self.dma_tot[sk])
            self.dma_tot[sk] += 16
            o.sem, o.inc = sk, 16
            done = (sk, self.dma_tot[sk])
            if is_out:
                self.out_dmas.append(done)
        else:
            self.cnt[eng] += 1
            o.sem, o.inc = eng, 1
            done = (eng, self.cnt[eng])
        m = {}
        for sk, v in o.waits:
            m[sk] = max(m.get(sk, 0), v)
        o.waits = list(m.items())
        for k in r:
            self.readers.setdefault(k, []).append(done)
        for k in w:
            self.lastw[k] = done
            self.readers[k] = []
        self.ops.append(o)
        return o

    def barrier(self):
        allk = [(e, self.cnt[e]) for e in ["pe", "act", "dve", "pool"] if self.cnt[e] > 0]
        allk += [(k, v) for k, v in self.dma_tot.items() if v > 0]
        for eng in ENGS:
            o = Op(eng, None)
            for sk, v in allk:
                if sk == eng and eng == "pe":
                    continue
                if self.known[eng].get(sk, 0) < v:
                    self.known[eng][sk] = v
                    o.waits.append((sk, v))
            self.ops.append(o)
        self.lastw = {}
        self.readers = {}

    def dma(self, q, out, in_, r=(), w=(), is_out=False):
        return self.op(q, lambda e: e.dma_start(out=out, in_=in_), r=r, w=w, dma=True, is_out=is_out)

    def emit(self):
        nc = self.nc
        fin = list(self.out_dmas)
        with nc.Block() as block:
            for eng in ENGS:
                ops = [o for o in self.ops if o.eng == eng]

                def body(e, ops=ops, eng=eng):
                    for o in ops:
                        for sk, v in o.waits:
                            e.wait_ge(self.semh[sk], v)
                        if o.fn is None:
                            continue
                        ins = o.fn(e)
                        ins.then_inc(self.semh[o.sem], o.inc)
                    if eng == "sp":
                        m = {}
                        for sk, v in fin:
                            m[sk] = max(m.get(sk, 0), v)
                        for sk, v in self.dma_tot.items():
                            if v > 0:
                                m[sk] = max(m.get(sk, 0), v)
                        for sk, v in m.items():
                            e.wait_ge(self.semh[sk], v)
                        for ce in ["pe", "act", "dve", "pool"]:
                            if self.cnt[ce] > 0:
                                e.wait_ge(self.semh[ce], self.cnt[ce])

                name = {"pe": "tensor", "act": "scalar", "dve": "vector", "pool": "gpsimd", "sp": "sync"}[eng]
                getattr(block, name)(body)
        self.stack.close()


D = 1024
DFF = 2816
NJ = DFF // 128
NMOD = 9
DIN = 6656
COL_U, COL_K, COL_V, COL_Q, COL_Z, COL_B, COL_C, COL_G = 0, 512, 1024, 1536, 2048, 2560, 3072, 3584
ALPHA = (2.0 * 2) ** 0.25
EPS = 1e-5
NT_LAT = 16
NT_CTX = 2
NT = NT_LAT + NT_CTX
NCH = 288
STW = 292
NEXP = 42
TWO_PI = 6.283185307179586
CW1 = 6.28125
CW2 = TWO_PI - 6.28125
RN = 66000


class StopBuild(Exception):
    pass


class Core:
    def __init__(self, nc, P):
        self.nc = nc
        self.P = P
        sb = P.sb
        self.modbc = sb("modbc", [128, 6, D], F32)
        self.lnp = sb("lnp", [128, 2, D], F32)
        self.idb = sb("idb", [128, 128], BF16)
        self.idf = sb("idf", [128, 128], F32)
        self.XB = sb("XB", [128, 6, D], F32)
        self.hb = sb("hb", [128, 2, D], BF16)
        self.tmpf = sb("tmpf", [128, 2, D], F32)
        self.zt = sb("zt", [128, 2, D], F32)
        self.stt = sb("stt", [128, 2, 2, 6], F32)
        self.mv = sb("mv", [128, 2, 8], F32)
        self.R = sb("R", [128, RN], BF16)
        self.psb = [P.ps("psb%d" % i, [128, 1024], BF16) for i in range(2)]
        self.psf = [P.ps("psf%d" % i, [128, 512], F32) for i in range(6)]
        self.xslot = 0

    def chk(self, n):
        if getattr(self, 's5stop', None) == n:
            raise StopBuild()

    def tt(self, eng, out, in0, in1, op, r, w):
        return self.P.op(eng, lambda e: e.tensor_tensor(out=out, in0=in0, in1=in1, op=op), r=r, w=w)

    def ts(self, eng, out, in0, s1, s2, op0, op1, r, w):
        return self.P.op(eng, lambda e: e.tensor_scalar(out=out, in0=in0, scalar1=s1, scalar2=s2, op0=op0, op1=op1), r=r, w=w)

    def stt_(self, eng, out, in0, scalar, in1, op0, op1, r, w):
        return self.P.op(eng, lambda e: e.scalar_tensor_tensor(out=out, in0=in0, scalar=scalar, in1=in1, op0=op0, op1=op1), r=r, w=w)

    def act(self, out, in_, func, r, w, scale=1.0, bias=0.0):
        return self.P.op("act", lambda e: e.activation(out=out, in_=in_, func=func, bias=bias, scale=scale), r=r, w=w)

    def cp(self, eng, out, in_, r, w):
        if eng == "act":
            return self.act(out, in_, AF.Copy, r, w)
        return self.P.op(eng, lambda e: e.tensor_copy(out=out, in_=in_), r=r, w=w)

    def Rf(self, off_bf16, n_f32):
        return self.R[:, off_bf16:off_bf16 + 2 * n_f32].bitcast(F32)

    def load_consts(self, idn):
        P = self.P
        P.dma("pool", self.idb[:], idn, w=["idb"])
        P.dma("sp", self.idf[:], idn, w=["idf"])

    def xload(self, src_rows, dkey=None):
        s = self.xslot
        self.xslot = (s + 1) % 6
        q = "sp" if s % 2 == 0 else "act"
        self.P.dma(q, self.XB[:, s, :], src_rows, r=[dkey] if dkey else [], w=[("XB", s)])
        return s

    def mod_phase(self, cvec, wmod, bmodT, modscr):
        P = self.P
        NWB = 4
        Rf = self.Rf(0, 17408)
        wm = [Rf[:, i * 4096:(i + 1) * 4096].rearrange("p (k c) -> p k c", k=8) for i in range(NWB)]
        B0 = 16384
        sc = Rf[:, B0:B0 + 16].rearrange("p (k v) -> p k v", k=8)
        bm = Rf[:, B0 + 16:B0 + 88]
        modT = Rf[:, B0 + 128:B0 + 272].rearrange("p (j v) -> p j v", j=72)
        rows = Rf[:, B0 + 320:B0 + 576]
        psm = self.psf[0][:, 0:144]
        P.dma("sp", sc, cvec, w=["sc"])
        P.dma("sp", bm, bmodT, w=["bm"])
        self.act(sc, sc, AF.Silu, r=["sc"], w=["sc"])
        for cc in range(18):
            b = cc % NWB
            q = "sp" if cc % 2 == 0 else "act"
            P.dma(q, wm[b], wmod[:, cc * 512:(cc + 1) * 512].rearrange("(k p) c -> p k c", p=128), w=[("wm", b)])

            def mm(e, b=b, cc=cc):
                ins = None
                for j in range(4):
                    col = (cc * 4 + j) * 2
                    for k in range(8):
                        ins = e.matmul(psm[:, col:col + 2], lhsT=wm[b][:, k, j * 128:(j + 1) * 128],
                                       rhs=sc[:, k, :], start=(k == 0), stop=(k == 7))
                return ins
            P.op("pe", mm, r=[("wm", b), "sc"], w=[("ps", 2)])
        psv = psm.rearrange("p (j v) -> p j v", j=72)
        for v in range(2):
            self.tt("dve", modT[:, :, v], psv[:, :, v], bm, ALU.add, r=[("ps", 2), "bm"], w=["modT"])
        for idx in (1, 4, 7):
            sl = modT[:, idx * 8:(idx + 1) * 8, :]
            P.op("dve", lambda e, sl=sl: e.tensor_scalar_add(out=sl, in0=sl, scalar1=1.0), r=["modT"], w=["modT"])
        for idx in (2, 8):
            sl = modT[:, idx * 8:(idx + 1) * 8, :]
            P.op("dve", lambda e, sl=sl: e.tensor_scalar_mul(out=sl, in0=sl, scalar1=0.5), r=["modT"], w=["modT"])
        mflat = Rf[:, B0 + 128:B0 + 272]
        pst = self.psf[1]
        P.op("pe", lambda e: e.transpose(pst[:, 0:128], mflat[:, 0:128], self.idf[:]), r=["modT", "idf"], w=[("ps", 3)])
        P.op("pe", lambda e: e.transpose(pst[0:16, 128:256], mflat[:, 128:144], self.idf[:]), r=["modT", "idf"], w=[("ps", 3)])
        self.cp("dve", rows[:, 0:128], pst[:, 0:128], r=[("ps", 3)], w=["rows"])
        self.cp("dve", rows[0:16, 128:256], pst[0:16, 128:256], r=[("ps", 3)], w=["rows"])
        ms = modscr.rearrange("j v f -> (j v) f")
        P.dma("sp", ms[0:128, :], rows[:, 0:128], r=["rows"], w=["modscr"], is_out=True)
        P.dma("sp", ms[128:144, :], rows[0:16, 128:256], r=["rows"], w=["modscr"], is_out=True)

    def load_mod(self, modscr, idxs, q="sp"):
        P = self.P
        for v in range(2):
            for i, idx in enumerate(idxs):
                dst = self.modbc[:, v * 3 + i, :].rearrange("p (j f) -> p j f", j=8)
                src = modscr[idx * 8:(idx + 1) * 8, v:v + 1, :].rearrange("j v f -> v j f").to_broadcast([128, 8, 128])
                P.dma(q, dst, src, r=["modscr"], w=[("modbc", v * 3 + i)])

    def load_ln(self, lng, lnb, s, q="act"):
        P = self.P
        P.dma(q, self.lnp[:, 0, :], lng[s:s + 1, :].to_broadcast([128, D]), w=[("lnp", 0)])
        P.dma(q, self.lnp[:, 1, :], lnb[s:s + 1, :].to_broadcast([128, D]), w=[("lnp", 1)])

    def modulate_T(self, xs, v, hT, pos, slot):
        P = self.P
        tm = self.tmpf[:, slot, :]
        hb = self.hb[:, slot, :]
        X, xk = (self.XB[:, xs, :], ("XB", xs)) if isinstance(xs, int) else xs
        self.tt("dve", tm, X, self.modbc[:, v * 3 + 1, :], ALU.mult, r=[xk, ("modbc", v * 3 + 1)], w=[("tmpf", slot)])
        self.tt("dve", hb, tm, self.modbc[:, v * 3 + 0, :], ALU.add, r=[("tmpf", slot), ("modbc", v * 3 + 0)], w=[("hb", slot)])
        pb = self.psb[slot]

        def tr(e):
            ins = None
            for k in range(8):
                ins = e.transpose(pb[:, k * 128:(k + 1) * 128], hb[:, k * 128:(k + 1) * 128], self.idb[:])
            return ins
        P.op("pe", tr, r=[("hb", slot), "idb"], w=[("ps", slot)])
        self.act(hT[:, :, pos * 128:(pos + 1) * 128], pb[:, :].rearrange("p (k c) -> p k c", k=8), AF.Copy,
                 r=[("ps", slot)], w=[("hT", pos)])

    def zbuf(self, i):
        i = i % 4
        if i < 2:
            return self.zt[:, i, :], ("zt", i)
        return self.tmpf[:, i - 2, :], ("tmpf", i - 2)

    def ln_evac(self, v, gate_i, ps_banks, zi):
        z, zk = self.zbuf(zi)
        gt = self.modbc[:, v * 3 + gate_i, :]
        for dh in range(2):
            b = ps_banks[dh]
            self.tt("dve", z[:, dh * 512:(dh + 1) * 512], self.psf[b - 2][:, :], gt[:, dh * 512:(dh + 1) * 512], ALU.mult,
                    r=[("ps", b), ("modbc", v * 3 + gate_i)], w=[zk])

    def ln_rest(self, xs, zi, slot):
        P = self.P
        z, zk = self.zbuf(zi)
        st = self.stt[:, slot]
        mv = self.mv[:, slot, :]
        X, xk = (self.XB[:, xs, :], ("XB", xs)) if isinstance(xs, int) else xs
        self.stt_("dve", z, X, ALPHA, z, ALU.mult, ALU.add, r=[xk, zk], w=[zk])
        for c in range(2):
            P.op("dve", lambda e, c=c: e.bn_stats(out=st[:, c, :], in_=z[:, c * 512:(c + 1) * 512]),
                 r=[zk], w=[("stt", slot, c)])
        P.op("dve", lambda e: e.bn_aggr(out=mv[:, 0:2], in_=st), r=[("stt", slot, 0), ("stt", slot, 1)], w=[("mv", slot)])
        P.op("dve", lambda e: e.tensor_scalar_add(out=mv[:, 4:5], in0=mv[:, 1:2], scalar1=EPS), r=[("mv", slot)], w=[("mv", slot)])
        self.act(mv[:, 5:6], mv[:, 4:5], AF.Sqrt, r=[("mv", slot)], w=[("mv", slot)])
        P.op("dve", lambda e: e.reciprocal(out=mv[:, 2:3], in_=mv[:, 5:6]), r=[("mv", slot)], w=[("mv", slot)])
        self.stt_("dve", mv[:, 3:4], mv[:, 0:1], -1.0, mv[:, 2:3], ALU.mult, ALU.mult, r=[("mv", slot)], w=[("mv", slot)])
        self.act(z, z, AF.Identity, r=[zk, ("mv", slot)], w=[zk], scale=mv[:, 2:3], bias=mv[:, 3:4])
        self.tt("pool", z, z, self.lnp[:, 0, :], ALU.mult, r=[zk, ("lnp", 0)], w=[zk])
        self.tt("pool", X, z, self.lnp[:, 1, :], ALU.add, r=[zk, ("lnp", 1)], w=[xk])

    def ln_epilogue(self, xs, v, gate_i, ps_banks, slot):
        self.ln_evac(v, gate_i, ps_banks, slot)
        self.ln_rest(xs, slot, slot)

    def ffn_phase(self, wg, wu, wd, xin, xout, blocks, xkey_in=None, xkey_out="xout"):
        P = self.P
        R = self.R
        TB = 768
        hT = R[:, 0:8 * TB].rearrange("p (k t) -> p k t", k=8)
        o = 8 * TB
        actT = R[:, o:o + NJ * TB].rearrange("p (j t) -> p j t", j=NJ)
        o += NJ * TB
        wgu = [[R[:, o + (b * 2 + i) * 2048: o + (b * 2 + i + 1) * 2048].rearrange("p (k c) -> p k c", k=8)
                for i in range(2)] for b in range(2)]
        o += 4 * 2048
        wdr = R[:, o:o + NJ * 1024].rearrange("p (j c) -> p j c", j=NJ)
        o += NJ * 1024
        wds = [self.Rf(o + b * 4096, 2048).rearrange("p (j c) -> p j c", j=2) for b in range(2)]
        o += 2 * 4096
        sgs = [R[:, o + b * 768: o + (b + 1) * 768].bitcast(F32) for b in range(2)]
        o += 2 * 768
        assert o <= RN
        wgc = 0
        zrot = 0
        for bi, blk in enumerate(blocks):
            nb = len(blk)
            for pos, t in enumerate(blk):
                v = 0 if t < NT_LAT else 1
                s = self.xload(xin[t * 128:(t + 1) * 128, :], dkey=(xkey_in, t) if xkey_in else None)
                self.modulate_T(s, v, hT, pos, pos % 2)
            groups = [(g0, min(g0 + 3, nb)) for g0 in range(0, nb, 3)]
            for jj in range(NJ // 2):
                b = wgc % 2
                wgc += 1
                for i, wsrc in enumerate((wg, wu)):
                    P.dma("pool", wgu[b][i], wsrc[:, jj * 256:(jj + 1) * 256].rearrange("(k p) c -> p k c", p=128),
                          w=[("wgu", b, i)])
                if bi == 0:
                    P.dma("sp" if jj % 2 == 0 else "act", wds[jj % 2], wd[jj * 256:(jj + 1) * 256, :].rearrange("(j p) c -> p j c", p=128),
                          w=[("wds", jj % 2)])
                    self.cp("act", wdr[:, 2 * jj:2 * jj + 2, :], wds[jj % 2], r=[("wds", jj % 2)], w=[("wdr", jj)])
                for jl in range(2):
                    j = jj * 2 + jl
                    for gi, (g0, g1) in enumerate(groups):
                        n = (g1 - g0) * 128
                        pg = self.psf[(gi % 2) * 2]
                        pu = self.psf[(gi % 2) * 2 + 1]
                        bg = 2 + (gi % 2) * 2
                        for i, pp in enumerate((pg, pu)):
                            def mm(e, i=i, pp=pp, b=b, jl=jl, g0=g0, n=n):
                                ins = None
                                for k in range(8):
                                    ins = e.matmul(pp[:, 0:n], lhsT=wgu[b][i][:, k, jl * 128:(jl + 1) * 128],
                                                   rhs=hT[:, k, g0 * 128:g0 * 128 + n], start=(k == 0), stop=(k == 7))
                                return ins
                            P.op("pe", mm, r=[("wgu", b, i)] + [("hT", p_) for p_ in range(g0, g1)], w=[("ps", bg + i)])
                        sg = sgs[gi % 2]
                        self.act(sg[:, 0:n], pg[:, 0:n], AF.Silu, r=[("ps", bg)], w=[("sgs", gi % 2)])
                        self.tt("dve", actT[:, j, g0 * 128:g0 * 128 + n], sg[:, 0:n], pu[:, 0:n], ALU.mult,
                                r=[("sgs", gi % 2), ("ps", bg + 1)], w=[("act", j, gi)])
            for (g0, g1) in groups:
                gi = g0 // 3
                for jj in range(NJ // 2):
                    def mm(e, jj=jj, g0=g0, g1=g1):
                        ins = None
                        for jl in range(2):
                            j = jj * 2 + jl
                            for p_ in range(g0, g1):
                                for dh in range(2):
                                    ins = e.matmul(self.psf[(p_ - g0) * 2 + dh][:, :], lhsT=actT[:, j, p_ * 128:(p_ + 1) * 128],
                                                   rhs=wdr[:, j, dh * 512:(dh + 1) * 512],
                                                   start=(j == 0), stop=(j == NJ - 1))
                        return ins
                    P.op("pe", mm, r=[("wdr", jj)] + [("act", jj * 2 + jl, gi) for jl in range(2)],
                         w=[("ps", 2 + (p_ - g0) * 2 + dh) for p_ in range(g0, g1) for dh in range(2)])
                xs_ = {}
                for p_ in range(g0, g1):
                    t = blk[p_]
                    v = 0 if t < NT_LAT else 1
                    bks = [2 + (p_ - g0) * 2, 3 + (p_ - g0) * 2]
                    xs_[p_] = self.xload(xin[t * 128:(t + 1) * 128, :], dkey=(xkey_in, t) if xkey_in else None)
                    self.ln_evac(v, 2, bks, zrot + (p_ - g0))
                for p_ in range(g0, g1):
                    t = blk[p_]
                    self.ln_rest(xs_[p_], zrot + (p_ - g0), p_ % 2)
                    P.dma("sp", xout[t * 128:(t + 1) * 128, :], self.XB[:, xs_[p_], :], r=[("XB", xs_[p_])],
                          w=[(xkey_out, t)], is_out=True)
                zrot += (g1 - g0)

    def make_hT(self, hT, entries):
        for i, (src, v, pos, dkey) in enumerate(entries):
            s = self.xload(src, dkey=dkey)
            self.modulate_T(s, v, hT, pos, i % 2)

    def s5_phase(self, x1, win, s5lam, s5b, s5c, s5exp, s5d, maskLU, wglu, mode, elist=None, aflag=None, eout=None,
                 xkey="x1", stio=None, utio=None, tabio=None):
        P = self.P
        R = self.R
        NG = 32
        UT8 = self.Rf(0, 9216).rearrange("p (g c) -> p g c", g=NG)
        STo = 18432
        ST = self.Rf(STo, 9344).rearrange("p (c g w) -> p c g w", c=2, g=16)
        SMo = 37120
        sm = self.Rf(SMo, 616)
        gT = R[:, 38352:47568].rearrange("p (k t) -> p k t", k=4)
        Gt32 = self.Rf(47568, 3072).rearrange("p (b t c) -> p b t c", b=3, t=8)
        Gb = R[:, 53712:56784].rearrange("p (b t c) -> p b t c", b=3, t=8)
        HBf = self.hb[:, :, :].rearrange("p a d -> p (a d)").bitcast(F32)
        yaT = R[:, 56784:66000].rearrange("p (k t) -> p k t", k=4)
        hTp = R[:, STo:STo + 8192].rearrange("p (k t) -> p k t", k=8)
        Ut = self.Rf(STo + 8192, 4096).rearrange("p (g j h) -> p g j h", g=32, j=8)
        wU = R[:, 38352:38352 + 4096].rearrange("p (k c) -> p k c", k=8)
        P.barrier()
        if mode == "A":
            P.dma("pool", wU, win[:, COL_U:COL_U + 512].rearrange("(k p) c -> p k c", p=128), w=["wU"])
            blkdefs = [(0, 8, 0, 128), (8, 8, 128, 128), (16, 2, 256, 32)]
            for (t0, ntl, cb, ncn) in blkdefs:
                ents = []
                for i in range(ntl):
                    t = t0 + i
                    ents.append((x1[t * 128:(t + 1) * 128, :], 0 if t < NT_LAT else 1, i, (xkey, t)))
                self.make_hT(hTp, ents)
                ntok = ntl * 128
                for j in range(8):
                    pp = self.psf[j % 2]

                    def mm(e, j=j, pp=pp, ncn=ncn, ntok=ntok):
                        ins = None
                        for k in range(8):
                            ins = e.matmul(pp[0:ncn, :], lhsT=hTp[:, k, j:ntok:8], rhs=wU[:, k, :], start=(k == 0), stop=(k == 7))
                        return ins
                    P.op("pe", mm, r=["wU"] + [("hT", i) for i in range(ntl)], w=[("ps", 2 + j % 2)])
                    self.cp("act" if j % 2 == 0 else "dve", Ut[0:ncn, :, j, :], pp[0:ncn, :].rearrange("p (g h) -> p g h", g=32),
                            r=[("ps", 2 + j % 2)], w=[("Ut", j)])
                for g0 in range(0, NG, 4):
                    pq = self.psf[2 + (g0 // 4) % 2]

                    def tr(e, g0=g0, pq=pq, ncn=ncn):
                        ins = None
                        for gg in range(4):
                            g = g0 + gg
                            ins = e.transpose(pq[:, gg * 128:gg * 128 + ncn], Ut[0:ncn, g].rearrange("p j h -> p (j h)"), self.idf[0:ncn, 0:ncn])
                        return ins
                    P.op("pe", tr, r=[("Ut", j) for j in range(8)] + ["idf"], w=[("ps", 4 + (g0 // 4) % 2)])
                    self.cp("act" if (g0 // 4) % 2 == 0 else "dve", UT8[:, g0:g0 + 4, cb:cb + ncn],
                            pq[:, :].rearrange("p (g c) -> p g c", g=4)[:, :, 0:ncn],
                            r=[("ps", 4 + (g0 // 4) % 2)], w=[("UT8", g0 // 4, cb)])
            P.barrier()
        if mode == "A":
            P.dma("sp", utio, self.Rf(0, 9216), is_out=True)
        else:
            P.dma("sp", self.Rf(0, 9216), utio, w=["UT8in"])
            P.barrier()
        XBf = self.XB[:, :, :].rearrange("p a d -> p (a d)")
        o = [0]

        def xa(n):
            a = XBf[:, o[0]:o[0] + n]
            o[0] += n
            return a
        NTB = NG * NEXP
        Tr = xa(NTB).rearrange("p (g e) -> p g e", g=NG)
        Ti = xa(NTB).rearrange("p (g e) -> p g e", g=NG)
        Bb = xa(1024).rearrange("p (c g h) -> p c g h", c=2, g=NG)
        Cc = xa(1024).rearrange("p (c g h) -> p c g h", c=2, g=NG)
        lam3 = xa(96).rearrange("p (a g) -> p a g", a=3)
        dtv, ar, ai = xa(32), xa(32), xa(32)
        exps = xa(NEXP + 2)
        qr, qi, den, t32a, t32b = xa(32), xa(32), xa(32), xa(32), xa(32)
        mLU = xa(256).rearrange("p (a c) -> p a c", a=2)
        Dg = xa(32)
        cst = xa(8)
        af = xa(4)
        El = sm[:, 226:418].rearrange("p (s c g) -> p s c g", s=3, c=2)
        y8off = o[0]
        Y8 = [xa(288), xa(288)]
        assert o[0] <= 6144
        STf = self.Rf(STo, 9344)
        argr_f = STf[:, 0:NTB]
        argi_f = STf[:, NTB:2 * NTB]
        argr = argr_f.rearrange("p (g e) -> p g e", g=NG)
        argi = argi_f.rearrange("p (g e) -> p g e", g=NG)
        mag = STf[:, 2 * NTB:3 * NTB]
        kf = STf[:, 0:NTB]
        ki = STf[:, 3 * NTB:4 * NTB].bitcast(mybir.dt.int32)
        rr = STf[:, 4 * NTB:5 * NTB]
        trg = STf[:, 5 * NTB:6 * NTB]
        btmp = STf[:, 6 * NTB:6 * NTB + 1024].rearrange("p (c g h) -> p c g h", c=2, g=NG)
        assert 6 * NTB + 1024 <= 9344
        if mode == "B":
            P.dma("sp", XBf[:, 0:6144], tabio, w=["tab"])
            P.dma("sp", af[:, 0:3], aflag, r=["tab"], w=["af"])
            P.dma("act", El, elist, w=["El"])
            P.barrier()
        else:
            P.dma("sp", lam3, s5lam, w=["lam3"])
            P.dma("sp", Bb, s5b, w=["Bb"])
            P.dma("act", Cc, s5c, w=["Cc"])
            P.dma("sp", exps[:, 0:NEXP], s5exp, w=["exps"])
            P.dma("act", mLU, maskLU, w=["mLU"])
            P.dma("sp", Dg, s5d, w=["Dg"])
            if mode == "B":
                P.dma("sp", af[:, 0:3], aflag, w=["af"])
                P.dma("act", El, elist, w=["El"])
            P.op("dve", lambda e: e.memset(cst[:, 0:1], -3.1415925), w=["cst"])
            self.act(dtv, lam3[:, 2, :], AF.Exp, r=["lam3"], w=["dtv"])
            self.tt("dve", ar, lam3[:, 0, :], dtv, ALU.mult, r=["lam3", "dtv"], w=["ar"])
            self.tt("dve", ai, lam3[:, 1, :], dtv, ALU.mult, r=["lam3", "dtv"], w=["ai"])
            self.tt("dve", argr, ar[:, :, None].to_broadcast([128, NG, NEXP]), exps[:, None, 0:NEXP].to_broadcast([128, NG, NEXP]),
                    ALU.mult, r=["ar", "exps"], w=["argr"])
            self.tt("dve", argi, ai[:, :, None].to_broadcast([128, NG, NEXP]), exps[:, None, 0:NEXP].to_broadcast([128, NG, NEXP]),
                    ALU.mult, r=["ai", "exps"], w=["argi"])
            self.act(mag, argr_f, AF.Exp, r=["argr"], w=["mag"])

            def sincos(dst, shift, key):
                self.ts("dve", kf, argi_f, shift, 1.0 / TWO_PI, ALU.add, ALU.mult, r=["argi", "mag"], w=["kf", "argr"])
                self.cp("dve", ki, kf, r=["kf"], w=["ki"])
                self.cp("dve", kf, ki, r=["ki"], w=["kf"])
                P.op("dve", lambda e: e.tensor_scalar_add(out=rr, in0=argi_f, scalar1=shift), r=["argi"], w=["rr"])
                self.stt_("dve", rr, kf, -CW1, rr, ALU.mult, ALU.add, r=["kf", "rr"], w=["rr"])
                self.stt_("dve", rr, kf, -CW2, rr, ALU.mult, ALU.add, r=["kf", "rr"], w=["rr"])
                self.ts("dve", rr, rr, 3.1415925, -3.1415925, ALU.min, ALU.max, r=["rr"], w=["rr"])
                self.act(dst, rr, AF.Sin, r=["rr"], w=[key])
            Tr_f = XBf[:, 0:NTB]
            Ti_f = XBf[:, NTB:2 * NTB]
            sincos(trg, 1.5707963267948966, "trg")
            self.tt("dve", Tr_f, mag, trg, ALU.mult, r=["mag", "trg"], w=["Tr"])
            sincos(trg, 0.0, "trg")
            self.tt("dve", Ti_f, mag, trg, ALU.mult, r=["mag", "trg"], w=["Ti"])
            L1r, L1i = t32a, t32b
            self.cp("dve", L1r[0:64, :], Tr[0:64, :, 8], r=["Tr"], w=["L1r"])
            self.cp("dve", L1r[64:128, :], Tr[64:128, :, 1], r=["Tr"], w=["L1r"])
            self.cp("dve", L1i[0:64, :], Ti[0:64, :, 8], r=["Ti"], w=["L1i"])
            self.cp("dve", L1i[64:128, :], Ti[64:128, :, 1], r=["Ti"], w=["L1i"])
            lr, li = lam3[:, 0, :], lam3[:, 1, :]
            P.op("dve", lambda e: e.tensor_scalar_add(out=L1r, in0=L1r, scalar1=-1.0), r=["L1r"], w=["L1r"])
            self.tt("dve", den, lr, lr, ALU.mult, r=["lam3"], w=["den"])
            self.tt("dve", qr, li, li, ALU.mult, r=["lam3"], w=["qr"])
            self.tt("dve", den, den, qr, ALU.add, r=["den", "qr"], w=["den"])
            P.op("dve", lambda e: e.reciprocal(out=den, in_=den), r=["den"], w=["den"])
            self.tt("dve", qr, L1r, lr, ALU.mult, r=["L1r", "lam3"], w=["qr"])
            self.tt("dve", qi, L1i, li, ALU.mult, r=["L1i", "lam3"], w=["qi"])
            self.tt("dve", qr, qr, qi, ALU.add, r=["qr", "qi"], w=["qr"])
            self.tt("dve", qi, L1i, lr, ALU.mult, r=["L1i", "lam3"], w=["qi"])
            self.tt("dve", L1i, L1r, li, ALU.mult, r=["L1r", "lam3"], w=["L1i"])
            self.tt("dve", qi, qi, L1i, ALU.subtract, r=["qi", "L1i"], w=["qi"])
            self.tt("dve", qr, qr, den, ALU.mult, r=["qr", "den"], w=["qr"])
            self.tt("dve", qi, qi, den, ALU.mult, r=["qi", "den"], w=["qi"])
            qrb = qr[:, :, None].to_broadcast([128, NG, 16])
            qib = qi[:, :, None].to_broadcast([128, NG, 16])
            self.tt("dve", btmp[:, 0], Bb[:, 0], qrb, ALU.mult, r=["Bb", "qr"], w=["btmp0"])
            self.tt("dve", btmp[:, 1], Bb[:, 1], qib, ALU.mult, r=["Bb", "qi"], w=["btmp1"])
            self.tt("dve", btmp[:, 0], btmp[:, 0], btmp[:, 1], ALU.subtract, r=["btmp0", "btmp1"], w=["btmp0"])
            self.tt("dve", btmp[:, 1], Bb[:, 1], qrb, ALU.mult, r=["Bb", "qr"], w=["btmp1"])
            self.tt("dve", Bb[:, 1], Bb[:, 0], qib, ALU.mult, r=["Bb", "qi"], w=["Bb"])
            self.tt("dve", Bb[:, 1], Bb[:, 1], btmp[:, 1], ALU.add, r=["Bb", "btmp1"], w=["Bb"])
            self.cp("dve", Bb[:, 0], btmp[:, 0], r=["btmp0", "Bb"], w=["Bb"])
            P.barrier()
            P.dma("sp", tabio, XBf[:, 0:6144], is_out=True)
        if getattr(self, "tdbg", None) is not None:
            P.dma("sp", self.tdbg, XBf[:, 0:3712], is_out=True)
        A8r2 = sm[:, 0:32].rearrange("p (c g) -> p c g", c=2)
        A8i = sm[:, 32:48]
        A8n = sm[:, 48:64]
        t1 = sm[:, 64:96].rearrange("p (c g) -> p c g", c=2)
        t2 = sm[:, 96:128].rearrange("p (c g) -> p c g", c=2)
        car = sm[:, 128:160].rearrange("p (c g) -> p c g", c=2)
        cm = sm[:, 160:192].rearrange("p (c g) -> p c g", c=2)
        A2r = sm[:, 192:208]
        A2i = sm[:, 208:224]
        hsel = sm[:, 224:226]
        P.op("dve", lambda e: e.memset(hsel, 0.0), w=["hsel"])
        P.op("dve", lambda e: e.memset(hsel[0:64, 0:1], 1.0), r=["hsel"], w=["hsel"])
        P.op("dve", lambda e: e.memset(hsel[64:128, 1:2], 1.0), r=["hsel"], w=["hsel"])
        TMf = self.tmpf[:, :, :].rearrange("p a d -> p (a d)")
        ZTf = self.zt[:, :, :].rearrange("p a d -> p (a d)")

        def gbuf(par):
            b = TMf if par == 0 else ZTf
            return dict(Wn=b[:, 0:256].rearrange("p (c m) -> p c m", c=2), Mo=b[:, 256:512].rearrange("p (c m) -> p c m", c=2),
                        Xn=b[:, 512:768].rearrange("p (c m) -> p c m", c=2), WT=b[:, 768:1024].rearrange("p (c m) -> p c m", c=2),
                        Mi=b[:, 1024:1152], tg=b[:, 1152:1408].rearrange("p (c m) -> p c m", c=2),
                        tg2=b[:, 1408:1536], XF=b[:, 1536:1792].rearrange("p (c m) -> p c m", c=2),
                        XBk=b[:, 1792:2048].rearrange("p (c m) -> p c m", c=2))

        def cmul(eng, dst, Pr, Pi, Qr, Qi, tmp, key, rk, neg_im=False):
            tk = key[:2] + "tg"
            self.tt(eng, dst[:, 0], Pr, Qr, ALU.mult, r=rk, w=[key + "0"])
            self.tt(eng, tmp[:, 0], Pi, Qi, ALU.mult, r=rk, w=[tk])
            self.tt(eng, dst[:, 0], dst[:, 0], tmp[:, 0], ALU.subtract, r=[key + "0", tk], w=[key + "0"])
            self.tt(eng, dst[:, 1], Pr, Qi, ALU.mult, r=rk, w=[key + "1"])
            self.tt(eng, tmp[:, 1], Pi, Qr, ALU.mult, r=rk, w=[tk])
            if neg_im:
                self.tt(eng, dst[:, 1], dst[:, 1], tmp[:, 1], ALU.add, r=[key + "1", tk], w=[key + "1"])
                d1 = dst[:, 1]
                self.P.op(eng, lambda e, d1=d1: e.tensor_scalar_mul(out=d1, in0=d1, scalar1=-1.0), r=[key + "1"], w=[key + "1"])
            else:
                self.tt(eng, dst[:, 1], dst[:, 1], tmp[:, 1], ALU.add, r=[key + "1", tk], w=[key + "1"])

        def v3(a):
            return a.rearrange("p (s h) -> p s h", s=8)

        def tab(T, g, e0):
            return T[:, g, e0:e0 + 8][:, :, None].to_broadcast([128, 8, 16])

        def par(B, c, g):
            return B[:, c, g, :][:, None, :].to_broadcast([128, 8, 16])

        def gen_W(g):
            gb = gbuf(g % 2)
            k = "g%d" % (g % 2)
            Wn = gb["Wn"]
            W3 = Wn.rearrange("p c (s h) -> p c s h", s=8)
            tg3 = gb["tg"].rearrange("p c (s h) -> p c s h", s=8)
            cmul("pool", W3, tab(Tr, g, 0), tab(Ti, g, 0), par(Bb, 0, g), par(Bb, 1, g), tg3, k + "W", ["Tr", "Ti", "Bb", "Bb"])
            pw = self.psf[4]

            def tr(e):
                e.transpose(pw[:, 0:128], Wn[:, 0, :], self.idf[:])
                return e.transpose(pw[:, 128:256], Wn[:, 1, :], self.idf[:])
            P.op("pe", tr, r=[k + "W0", k + "W1", "idf"], w=[("ps", 6)])
            self.cp("dve", gb["WT"].rearrange("p c m -> p (c m)"), pw[:, 0:256], r=[("ps", 6)], w=[k + "WT"])
            return gb

        def gen_M(g):
            gb = gbuf(g % 2)
            k = "g%d" % (g % 2)
            Mo3 = gb["Mo"].rearrange("p c (s h) -> p c s h", s=8)
            Xn3 = gb["Xn"].rearrange("p c (s h) -> p c s h", s=8)
            tg3 = gb["tg"].rearrange("p c (s h) -> p c s h", s=8)
            cmul("pool", Mo3, tab(Tr, g, 8), tab(Ti, g, 8), par(Cc, 0, g), par(Cc, 1, g), tg3, k + "Mo", ["Tr", "Ti", "Cc"], neg_im=True)
            cmul("pool", Xn3, tab(Tr, g, 16), tab(Ti, g, 16), par(Bb, 0, g), par(Bb, 1, g), tg3, k + "Xn", ["Tr", "Ti", "Bb", "Bb"])
            pw = self.psf[4]
            Mo, Xn = gb["Mo"], gb["Xn"]
            XF, XBk = gb["XF"], gb["XBk"]
            P.op("dve", lambda e: e.tensor_scalar_mul(out=XF, in0=Xn, scalar1=hsel[:, 0:1]), r=[k + "Xn0", k + "Xn1", "hsel"], w=[k + "XF"])
            P.op("dve", lambda e: e.tensor_scalar_mul(out=XBk, in0=Xn, scalar1=hsel[:, 1:2]), r=[k + "Xn0", k + "Xn1", "hsel"], w=[k + "XB"])

            def mm(e):
                ins = None
                for hf, Xm in enumerate((XF, XBk)):
                    e.matmul(pw[:, hf * 128:(hf + 1) * 128], lhsT=Xm[:, 0, :], rhs=Mo[:, 0, :], start=True, stop=False)
                    ins = e.matmul(pw[:, hf * 128:(hf + 1) * 128], lhsT=Xm[:, 1, :], rhs=Mo[:, 1, :], start=False, stop=True)
                return ins
            P.op("pe", mm, r=[k + "Mo0", k + "Mo1", k + "XF", k + "XB"], w=[("ps", 6)])
            Mi = gb["Mi"]
            self.tt("dve", gb["tg2"], pw[:, 0:128], mLU[:, 0, :], ALU.mult, r=[("ps", 6), "mLU"], w=[k + "tg2"])
            self.tt("dve", Mi, pw[:, 128:256], mLU[:, 1, :], ALU.mult, r=[("ps", 6), "mLU"], w=[k + "Mi"])
            self.tt("dve", Mi, Mi, gb["tg2"], ALU.add, r=[k + "Mi", k + "tg2"], w=[k + "Mi"])
            self.stt_("dve", Mi, self.idf[:], Dg[:, g:g + 1], Mi, ALU.mult, ALU.add, r=[k + "Mi", "idf", "Dg"], w=[k + "Mi"])
            return gb

        for hs in range(2):
            G0 = hs * 16
            self.cp("dve", A8r2[:, 0, :], Tr[:, G0:G0 + 16, 24], r=["Tr"], w=["A8r2"])
            self.cp("dve", A8r2[:, 1, :], Tr[:, G0:G0 + 16, 24], r=["Tr"], w=["A8r2"])
            self.cp("dve", A8i, Ti[:, G0:G0 + 16, 24], r=["Ti"], w=["A8i"])
            P.op("dve", lambda e, G0=G0: e.tensor_scalar_mul(out=A8n, in0=Ti[:, G0:G0 + 16, 24], scalar1=-1.0), r=["Ti"], w=["A8n"])
            self.cp("dve", A2r, Tr[:, G0:G0 + 16, 25], r=["Tr"], w=["A2r"])
            self.cp("dve", A2i, Ti[:, G0:G0 + 16, 25], r=["Ti"], w=["A2i"])
            if mode == "B":
                P.dma("sp", STf[:, 0:9344], stio[hs], w=[("ST", 0), ("ST", 1)])
            else:
                P.op("pool", lambda e: e.memset(STf[:, 0:9344], 0.0), w=[("ST", 0), ("ST", 1)])
            self.chk(1)
            for gl in (range(16) if mode == "A" else []):
                g = G0 + gl
                gb = gen_W(g)
                pr, pi = self.psf[(gl % 2) * 2], self.psf[(gl % 2) * 2 + 1]
                br = 2 + (gl % 2) * 2
                k = "g%d" % (g % 2)
                P.op("pe", lambda e, pr=pr, gb=gb, g=g: e.matmul(pr[:, 0:NCH], lhsT=gb["WT"][:, 0, :], rhs=UT8[:, g, :], start=True, stop=True),
                     r=[k + "WT"], w=[("ps", br)])
                P.op("pe", lambda e, pi=pi, gb=gb, g=g: e.matmul(pi[:, 0:NCH], lhsT=gb["WT"][:, 1, :], rhs=UT8[:, g, :], start=True, stop=True),
                     r=[k + "WT"], w=[("ps", br + 1)])
                for c, pp in enumerate((pr, pi)):
                    e1 = "act" if c == 0 else "dve"
                    self.cp(e1, ST[0:64, c, gl, 2:258], pp[0:64, 0:256], r=[("ps", br + c)], w=[("ST", 0)])
                    self.cp(e1, ST[0:64, c, gl, 260:292], pp[0:64, 256:288], r=[("ps", br + c)], w=[("ST", 0)])
                    self.cp(e1, ST[64:128, c, gl, 0:256], pp[64:128, 0:256], r=[("ps", br + c)], w=[("ST", 1)])
                    self.cp(e1, ST[64:128, c, gl, 258:290], pp[64:128, 256:288], r=[("ps", br + c)], w=[("ST", 1)])
            self.chk(2)
            def step(half, src, dst):
                eng = "dve" if half == 0 else "pool"
                sl = slice(half * 64, half * 64 + 64)
                h = "h%d" % half
                cur = ST[sl, :, :, src]
                nxt = ST[sl, :, :, dst]
                self.tt(eng, t1[sl], cur, A8r2[sl], ALU.mult, r=[("ST", half), "A8r2"], w=["t1" + h])
                self.tt(eng, t2[sl, 0, :], cur[:, 1, :], A8n[sl], ALU.mult, r=[("ST", half), "A8n"], w=["t2a" + h])
                self.tt(eng, t2[sl, 1, :], cur[:, 0, :], A8i[sl], ALU.mult, r=[("ST", half), "A8i"], w=["t2b" + h])
                self.tt(eng, t1[sl], t1[sl], t2[sl], ALU.add, r=["t1" + h, "t2a" + h, "t2b" + h], w=["t1" + h])
                self.tt(eng, nxt, nxt, t1[sl], ALU.add, r=["t1" + h, ("ST", half)], w=[("ST", half)])

            if mode == "A":
                for c in range(32):
                    step(0, 259 + c, 260 + c)
                    step(1, 290 - c, 289 - c)
            if mode == "B":
                self.chk(3)
                self.cp("dve", car[0:64], ST[0:64, :, :, 291], r=[("ST", 0)], w=["car"])
                self.cp("dve", car[64:128], ST[64:128, :, :, 258], r=[("ST", 1)], w=["car"])
                for s in range(3):
                    self.tt("dve", cm[:, 0, :], car[:, 0, :], A2r, ALU.mult, r=["car", "A2r"], w=["cm"])
                    self.tt("dve", t1[:, 0, :], car[:, 1, :], A2i, ALU.mult, r=["car", "A2i"], w=["t1h0", "t1h1"])
                    self.tt("dve", cm[:, 0, :], cm[:, 0, :], t1[:, 0, :], ALU.subtract, r=["cm", "t1h0", "t1h1"], w=["cm"])
                    self.tt("dve", cm[:, 1, :], car[:, 1, :], A2r, ALU.mult, r=["car", "A2r"], w=["cm"])
                    self.tt("dve", t1[:, 1, :], car[:, 0, :], A2i, ALU.mult, r=["car", "A2i"], w=["t1h0", "t1h1"])
                    self.tt("dve", cm[:, 1, :], cm[:, 1, :], t1[:, 1, :], ALU.add, r=["cm", "t1h0", "t1h1"], w=["cm"])
                    self.tt("dve", cm, cm, car, ALU.subtract, r=["cm", "car"], w=["cm"])
                    self.stt_("dve", car, cm, af[:, s:s + 1], car, ALU.mult, ALU.add, r=["cm", "car", "af"], w=["car"])
                    self.tt("dve", car, car, El[:, s, :, G0:G0 + 16], ALU.add, r=["car", "El"], w=["car"])
                self.cp("dve", ST[0:64, :, :, 1], car[0:64], r=["car"], w=[("ST", 0)])
                self.cp("dve", ST[64:128, :, :, 256], car[64:128], r=["car"], w=[("ST", 1)])
            self.chk(4)
            T1 = HBf[:, 0:512].rearrange("p (c g b) -> p c g b", c=2, g=16)
            T2 = HBf[:, 512:1024].rearrange("p (c g b) -> p c g b", c=2, g=16)
            CIN = XBf[:, y8off:y8off + 544].rearrange("p (c g b) -> p c g b", c=2, g=16)

            def cma(half, cur, nxt, ecol, t1v, t2v, kcur, knxt, ktmp):
                eng = "dve" if half == 0 else "pool"
                sl = slice(half * 64, half * 64 + 64)
                shp = list(cur.shape)
                crs = Tr[sl, G0:G0 + 16, ecol]
                cis = Ti[sl, G0:G0 + 16, ecol]
                if len(shp) == 4:
                    crb = crs[:, None, :, None].to_broadcast(shp)
                    cib = cis[:, None, :, None].to_broadcast(shp)
                else:
                    crb = crs[:, None, :].to_broadcast(shp)
                    cib = cis[:, None, :].to_broadcast(shp)
                self.tt(eng, t1v, cur, crb, ALU.mult, r=kcur + ["Tr"], w=[ktmp + "1"])
                self.tt(eng, t2v, cur, cib, ALU.mult, r=kcur + ["Ti"], w=[ktmp + "2"])
                self.tt(eng, t1v[:, 0], t1v[:, 0], t2v[:, 1], ALU.subtract, r=[ktmp + "1", ktmp + "2"], w=[ktmp + "1"])
                self.tt(eng, t1v[:, 1], t1v[:, 1], t2v[:, 0], ALU.add, r=[ktmp + "1", ktmp + "2"], w=[ktmp + "1"])
                self.tt(eng, nxt, nxt, t1v, ALU.add, r=[ktmp + "1"] + knxt, w=knxt)

            P.barrier()
            kS = [[("ST", 0)], [("ST", 1)]]
            kC = [[("CIN", 0)], [("CIN", 1)]]
            hsl = [slice(0, 64), slice(64, 128)]
            for j in (range(1, 16) if mode == "A" else []):
                cma(0, ST[hsl[0], :, :, j + 1:j + 242:16], ST[hsl[0], :, :, j + 2:j + 243:16], 24,
                    T1[hsl[0]], T2[hsl[0]], kS[0], kS[0], "HB0")
                jb = 15 - j
                cma(1, ST[hsl[1], :, :, jb + 1:jb + 242:16], ST[hsl[1], :, :, jb:jb + 241:16], 24,
                    T1[hsl[1]], T2[hsl[1]], kS[1], kS[1], "HB1")
            if mode == "A":
                P.dma("sp", stio[hs], STf[:, 0:9344], r=kS[0] + kS[1], is_out=True)
            self.cp("dve", CIN[hsl[0], :, :, 0], ST[hsl[0], :, :, 1], r=kS[0] + [("Y8", 0), ("Y8", 1)], w=kC[0])
            self.cp("pool", CIN[hsl[1], :, :, 15], ST[hsl[1], :, :, 256], r=kS[1] + [("Y8", 0), ("Y8", 1)], w=kC[1])
            for b in range(16):
                self.cp("dve", CIN[hsl[0], :, :, b + 1], ST[hsl[0], :, :, 16 * b + 17], r=kS[0], w=kC[0])
                cma(0, CIN[hsl[0], :, :, b], CIN[hsl[0], :, :, b + 1], 41, t1[hsl[0]], t2[hsl[0]], kC[0], kC[0], "t1h0")
                if b < 15:
                    bb_ = 15 - b
                    self.cp("pool", CIN[hsl[1], :, :, bb_ - 1], ST[hsl[1], :, :, 16 * bb_], r=kS[1], w=kC[1])
                    cma(1, CIN[hsl[1], :, :, bb_], CIN[hsl[1], :, :, bb_ - 1], 41, t1[hsl[1]], t2[hsl[1]], kC[1], kC[1], "t1h1")
            if mode == "A":
                self.cp("pool", CIN[hsl[1], :, :, 16], ST[hsl[1], :, :, 0], r=kS[1], w=kC[1])
                cma(1, CIN[hsl[1], :, :, 0], CIN[hsl[1], :, :, 16], 41, t1[hsl[1]], t2[hsl[1]], kC[1], kC[1], "t1h1")
                self.cp("dve", car[0:64], CIN[hsl[0], :, :, 16], r=kC[0], w=["car"])
                self.cp("dve", car[64:128], CIN[hsl[1], :, :, 16], r=kC[1], w=["car"])
                P.dma("sp", eout[hs], car, r=["car"], is_out=True)
                continue
            for j in range(16):
                cma(0, CIN[hsl[0], :, :, 0:16], ST[hsl[0], :, :, j + 2:j + 243:16], 26 + j,
                    T1[hsl[0]], T2[hsl[0]], kC[0], kS[0], "HB0")
                cma(1, CIN[hsl[1], :, :, 0:16], ST[hsl[1], :, :, j:j + 241:16], 26 + (15 - j),
                    T1[hsl[1]], T2[hsl[1]], kC[1], kS[1], "HB1")
            self.chk(5)
            for gl in range(16):
                g = G0 + gl
                gb = gen_M(g)
                if gl == 0:
                    self.chk(6)
                k = "g%d" % (g % 2)
                py = self.psf[(gl % 2) * 2]
                by = 2 + (gl % 2) * 2

                def mm(e, gb=gb, g=g, gl=gl, py=py):
                    ins = None
                    for (c0, c1, s0) in ((0, 256, 1), (256, 288, 259)):
                        n = c1 - c0
                        e.matmul(py[:, c0:c1], lhsT=gb["Mi"], rhs=UT8[:, g, c0:c1], start=True, stop=False)
                        e.matmul(py[:, c0:c1], lhsT=gb["Mo"][:, 0, :], rhs=ST[:, 0, gl, s0:s0 + n], start=False, stop=False)
                        ins = e.matmul(py[:, c0:c1], lhsT=gb["Mo"][:, 1, :], rhs=ST[:, 1, gl, s0:s0 + n], start=False, stop=True)
                    return ins
                P.op("pe", mm, r=[k + "Mi", k + "Mo0", k + "Mo1", ("ST", 0), ("ST", 1)], w=[("ps", by)])
                y8 = Y8[gl % 2]
                self.cp("act", y8, py[:, 0:NCH], r=[("ps", by)], w=[("Y8", gl % 2), ("CIN", 0), ("CIN", 1)])
                pt = self.psf[5]

                def tr(e, y8=y8):
                    e.transpose(pt[:, 0:128], y8[:, 0:128], self.idf[:])
                    e.transpose(pt[:, 128:256], y8[:, 128:256], self.idf[:])
                    return e.transpose(pt[0:32, 256:384], y8[:, 256:288], self.idf[:])
                P.op("pe", tr, r=[("Y8", gl % 2), "idf"], w=[("ps", 7)])
                if gl == 0:
                    self.chk(7)
                gq = g % 8
                for b in range(3):
                    rows = 128 if b < 2 else 32
                    self.cp("act", Gt32[0:rows, b, :, gq * 16:(gq + 1) * 16],
                            pt[0:rows, b * 128:(b + 1) * 128].rearrange("p (t h) -> p t h", t=8), r=[("ps", 7)], w=[("Gt", b)])
                if gq == 7:
                    kc = g // 8
                    for b in range(3):
                        rows = 128 if b < 2 else 32
                        xg = Gt32[0:rows, b].rearrange("p t c -> p (t c)")
                        ug = HBf[0:rows, :]
                        self.tt("dve", ug, xg, xg, ALU.mult, r=[("Gt", b), "HB01", "HB02", "HB11", "HB12"], w=["ug", "HB01", "HB02", "HB11", "HB12"])
                        self.ts("dve", ug, ug, 0.044715, 1.0, ALU.mult, ALU.add, r=["ug"], w=["ug"])
                        self.tt("dve", ug, ug, xg, ALU.mult, r=["ug", ("Gt", b)], w=["ug"])
                        self.act(ug, ug, AF.Sigmoid, r=["ug"], w=["ug"], scale=1.5957691216057308)
                        self.tt("dve", Gb[0:rows, b].rearrange("p t c -> p (t c)"), xg, ug, ALU.mult, r=["ug", ("Gt", b)], w=[("Gb", b)])
                        pb = self.psb[b % 2]

                        def tr2(e, b=b, rows=rows, pb=pb):
                            ins = None
                            for t in range(8):
                                ins = e.transpose(pb[:, t * 128:t * 128 + rows], Gb[0:rows, b, t, :], self.idb[0:rows, 0:rows])
                            return ins
                        P.op("pe", tr2, r=[("Gb", b), "idb"], w=[("ps", b % 2)])
                        dst = gT[:, kc, b * 1024:b * 1024 + rows * 8].rearrange("p (c t) -> p t c", t=8)
                        self.cp("dve", dst, pb[:, :].rearrange("p (t c) -> p t c", t=8)[:, :, 0:rows], r=[("ps", b % 2)], w=[("gT", kc, b)])
        if mode == "A":
            return
        self.chk(8)
        P.barrier()
        wgl = R[:, 0:2048].rearrange("p (k c) -> p k c", k=4)
        sgb = [R[:, 2048 + i * 1024:2048 + (i + 1) * 1024].bitcast(F32) for i in range(2)]
        P.dma("pool", wgl, wglu.rearrange("(k p) c -> p k c", p=128), w=["wgl"])
        ci = 0
        for oc in range(4):
            for (n0, n) in ((0, 512), (512, 512), (1024, 512), (1536, 512), (2048, 256)):
                pp = self.psf[ci % 2]

                def mm(e, oc=oc, n0=n0, n=n, pp=pp):
                    ins = None
                    for kc in range(4):
                        ins = e.matmul(pp[:, 0:n], lhsT=wgl[:, kc, oc * 128:(oc + 1) * 128], rhs=gT[:, kc, n0:n0 + n],
                                       start=(kc == 0), stop=(kc == 3))
                    return ins
                P.op("pe", mm, r=["wgl"], w=[("ps", 2 + ci % 2)])
                self.act(sgb[ci % 2][:, 0:n], pp[:, 0:n], AF.Sigmoid, r=[("ps", 2 + ci % 2)], w=[("sgb", ci % 2)])
                self.tt("dve", yaT[:, oc, n0:n0 + n], gT[:, oc, n0:n0 + n], sgb[ci % 2][:, 0:n], ALU.mult,
                        r=[("sgb", ci % 2)], w=[("yaT", oc, n0)])
                ci += 1
        P.barrier()

    HT_FULL = 22

    def build_hT_full(self, x1, xh, xkey="x1"):
        R = self.R
        hT = R[:, 0:22528].rearrange("p (k t) -> p k t", k=8)
        ents = []
        for i in range(2):
            ents.append((xh[i * 128:(i + 1) * 128, :], 0, i, None))
        for t in range(NT_LAT):
            ents.append((x1[t * 128:(t + 1) * 128, :], 0, 2 + t, (xkey, t)))
        for i in range(2):
            ents.append((xh[256 + i * 128:256 + (i + 1) * 128, :], 0, 18 + i, None))
        for i in range(2):
            t = NT_LAT + i
            ents.append((x1[t * 128:(t + 1) * 128, :], 1, 20 + i, (xkey, t)))
        self.make_hT(hT, ents)
        return hT

    def attn_phase(self, hT, win, biasT, maskT):
        P = self.P
        R = self.R
        o = 22528
        kT = R[:, o:o + 2816]; o += 2816
        qT = R[:, o:o + 2304]; o += 2304
        Va = R[:, o:o + 2860].rearrange("p (i h d) -> p i h d", i=22, h=2); o += 2860
        wk = R[:, o:o + 3072].rearrange("p (k c) -> p k c", k=8); o += 3072
        bb = R[:, o:o + 3584].rearrange("p (h j x c) -> p h j x c", h=2, j=7, x=2); o += 3584
        ybt = [R[:, o + i * 128:o + (i + 1) * 128].rearrange("p (h d) -> p h d", h=2) for i in range(2)]; o += 256
        rec = self.Rf(o, 4); o += 8
        assert o <= 38352
        ybT = R[:, 38352:47568].rearrange("p (k t) -> p k t", k=4)
        o = 47568
        bst = self.Rf(o, 1792).rearrange("p (h j c) -> p h j c", h=2, j=7); o += 3584
        mb = R[:, o:o + 3456].rearrange("p (i c) -> p i c", i=27); o += 3456
        ET = [R[:, o + i * 1024:o + (i + 1) * 1024] for i in range(2)]; o += 2048
        assert o <= 56784
        P.dma("pool", mb, maskT, w=["mb"])
        P.op("pool", lambda e: e.memset(Va[:, :, :, 64:65], 1.0), w=["Va1"])
        mcls = {0: (0, list(range(0, 6))), 1: (6, list(range(0, 5))), 14: (16, list(range(0, 5))), 15: (21, list(range(-1, 5)))}
        ev = 0
        for hp in range(4):
            for i, col in enumerate((COL_K, COL_V, COL_Q)):
                P.dma("pool", wk[:, :, i * 128:(i + 1) * 128],
                      win[:, col + hp * 128:col + (hp + 1) * 128].rearrange("(k p) c -> p k c", p=128), w=[("wk", i)])
            P.dma("sp", bst, biasT[hp], w=["bst"])
            self.cp("dve", bb[:, :, :, 0, :], bst, r=["bst"], w=["bb0"])
            self.tt("dve", bst, bst, bb[:, :, :, 0, :], ALU.subtract, r=["bst", "bb0"], w=["bst"])
            self.cp("dve", bb[:, :, :, 1, :], bst, r=["bst"], w=["bb1"])
            for (n0, n) in ((0, 512), (512, 512), (1024, 512), (1536, 512), (2048, 512), (2560, 256)):
                pp = self.psf[ev % 2]

                def mm(e, n0=n0, n=n, pp=pp):
                    ins = None
                    for k in range(8):
                        ins = e.matmul(pp[:, 0:n], lhsT=wk[:, k, 0:128], rhs=hT[:, k, n0:n0 + n], start=(k == 0), stop=(k == 7))
                    return ins
                P.op("pe", mm, r=[("wk", 0)], w=[("ps", 2 + ev % 2)])
                self.cp("act" if ev % 2 == 0 else "dve", kT[:, n0:n0 + n], pp[:, 0:n], r=[("ps", 2 + ev % 2)], w=["kT"])
                ev += 1
            for (h0, q0, n) in ((256, 0, 512), (768, 512, 512), (1280, 1024, 512), (1792, 1536, 512), (2560, 2048, 256)):
                pp = self.psf[ev % 2]

                def mm(e, h0=h0, n=n, pp=pp):
                    ins = None
                    for k in range(8):
                        ins = e.matmul(pp[:, 0:n], lhsT=wk[:, k, 256:384], rhs=hT[:, k, h0:h0 + n], start=(k == 0), stop=(k == 7))
                    return ins
                P.op("pe", mm, r=[("wk", 2)], w=[("ps", 2 + ev % 2)])
                P.op("dve", lambda e, q0=q0, n=n, pp=pp: e.tensor_scalar_mul(out=qT[:, q0:q0 + n], in0=pp[:, 0:n], scalar1=0.125),
                     r=[("ps", 2 + ev % 2)], w=["qT"])
                ev += 1
            for i0 in range(0, 22, 4):
                ni = min(4, 22 - i0)
                pp = self.psf[ev % 2]

                def mm(e, i0=i0, ni=ni, pp=pp):
                    ins = None
                    for ii in range(ni):
                        i = i0 + ii
                        for k in range(8):
                            ins = e.matmul(pp[:, ii * 128:(ii + 1) * 128], lhsT=hT[:, k, i * 128:(i + 1) * 128], rhs=wk[:, k, 128:256],
                                           start=(k == 0), stop=(k == 7))
                    return ins
                P.op("pe", mm, r=[("wk", 1)], w=[("ps", 2 + ev % 2)])
                self.cp("act" if ev % 2 == 0 else "dve", Va[:, i0:i0 + ni, :, 0:64],
                        pp[:, 0:ni * 128].rearrange("p (i h d) -> p i h d", i=ni, h=2), r=[("ps", 2 + ev % 2)], w=["Va"])
                ev += 1
            for m in range(18):
                po = self.psf[4 + m % 2]
                for hh in range(2):
                    sl = slice(hh * 64, hh * 64 + 64)
                    it = (m * 2 + hh) % 2
                    pa, pbk = self.psf[it * 2], self.psf[it * 2 + 1]
                    if m < 16:
                        moff, jos = mcls.get(m, (11, list(range(0, 5))))
                        chunks = [(m + jo, jo, moff + ji) for ji, jo in enumerate(jos)] + [(20, None, None), (21, None, None)]
                    else:
                        chunks = [(20, None, None), (21, None, None)]
                    nchk = len(chunks)

                    def mm(e, chunks=chunks, sl=sl, hh=hh, m=m, pa=pa, pbk=pbk):
                        ins = None
                        for c, (ti, jo, mi) in enumerate(chunks):
                            dst = (pa if c < 4 else pbk)[:, (c % 4) * 128:(c % 4 + 1) * 128]
                            ins = e.matmul(dst, lhsT=kT[sl, ti * 128:(ti + 1) * 128], rhs=qT[sl, m * 128:(m + 1) * 128],
                                           start=True, stop=(jo is None))
                            if jo is not None:
                                e.matmul(dst, lhsT=self.idb[:], rhs=bb[:, hh, jo + 1, 0, :], start=False, stop=False)
                                e.matmul(dst, lhsT=self.idb[:], rhs=bb[:, hh, jo + 1, 1, :], start=False, stop=False)
                                ins = e.matmul(dst, lhsT=self.idb[:], rhs=mb[:, mi, :], start=False, stop=True)
                        return ins
                    P.op("pe", mm, r=["kT", "qT", "bb0", "bb1", "mb", "idb"], w=[("ps", 2 + it * 2), ("ps", 3 + it * 2)])
                    n1 = min(nchk, 4) * 128
                    n2 = (nchk - 4) * 128
                    self.act(ET[it][:, 0:n1], pa[:, 0:n1], AF.Exp, r=[("ps", 2 + it * 2)], w=[("ET", it)])
                    if n2 > 0:
                        self.act(ET[it][:, 512:512 + n2], pbk[:, 0:n2], AF.Exp, r=[("ps", 3 + it * 2)], w=[("ET", it)])

                    def pv(e, chunks=chunks, hh=hh, it=it, po=po):
                        ins = None
                        for c, (ti, jo, mi) in enumerate(chunks):
                            ins = e.matmul(po[:, hh * 65:(hh + 1) * 65], lhsT=ET[it][:, c * 128:(c + 1) * 128], rhs=Va[:, ti, hh, :],
                                           start=(c == 0), stop=(c == len(chunks) - 1))
                        return ins
                    P.op("pe", pv, r=[("ET", it), "Va", "Va1"], w=[("ps", 6 + m % 2)])
                pov = po[:, 0:130].rearrange("p (h d) -> p h d", h=2)
                P.op("dve", lambda e, pov=pov: e.reciprocal(out=rec[:, 0:2], in_=pov[:, :, 64]), r=[("ps", 6 + m % 2)], w=["rec"])
                yb = ybt[m % 2]
                self.tt("dve", yb, pov[:, :, 0:64], rec[:, 0:2][:, :, None].to_broadcast([128, 2, 64]), ALU.mult,
                        r=[("ps", 6 + m % 2), "rec"], w=[("ybt", m % 2)])
                pt = self.psb[m % 2]
                P.op("pe", lambda e, yb=yb, pt=pt: e.transpose(pt[:, 0:128], yb.rearrange("p h d -> p (h d)"), self.idb[:]),
                     r=[("ybt", m % 2), "idb"], w=[("ps", m % 2)])
                self.cp("act", ybT[:, hp, m * 128:(m + 1) * 128], pt[:, 0:128], r=[("ps", m % 2)], w=[("ybT", hp, m)])

    def conv_phase(self, hT, win, cw, cflag):
        P = self.P
        R = self.R
        ycT = R[:, 47568:56784].rearrange("p (k t) -> p k t", k=4)
        o = 22528
        wz = R[:, o:o + 3072].rearrange("p (k c) -> p k c", k=8); o += 3072
        vv = self.Rf(o, 2052); o += 4104
        vc = self.Rf(o, 260); o += 520
        zs = self.Rf(o, 512); o += 1024
        yt = self.Rf(o, 512); o += 1024
        cwt = self.Rf(o, 12).rearrange("p (c i) -> p c i", c=4); o += 24
        cfl = self.Rf(o, 2); o += 8
        assert o <= 38352
        P.barrier()
        P.dma("sp", cwt, cw, w=["cwt"])
        P.dma("sp", cfl, cflag, w=["cfl"])
        P.op("dve", lambda e: e.memset(vc[:, :], 0.0), w=["vc"])
        for cc in range(4):
            for i, col in enumerate((COL_Z, COL_C, COL_B)):
                P.dma("pool", wz[:, :, i * 128:(i + 1) * 128],
                      win[:, col + cc * 128:col + (cc + 1) * 128].rearrange("(k p) c -> p k c", p=128), w=[("wz", i)])

            def zc(h0, n, dst, dkey):
                for i in range(2):
                    pp = self.psf[i]

                    def mm(e, i=i, pp=pp):
                        ins = None
                        for k in range(8):
                            ins = e.matmul(pp[:, 0:n], lhsT=wz[:, k, i * 128:(i + 1) * 128], rhs=hT[:, k, h0:h0 + n],
                                           start=(k == 0), stop=(k == 7))
                        return ins
                    P.op("pe", mm, r=[("wz", i)], w=[("ps", 2 + i)])
                self.cp("act", zs[:, 0:n], self.psf[0][:, 0:n], r=[("ps", 2)], w=["zs"])
                self.tt("dve", dst, zs[:, 0:n], self.psf[1][:, 0:n], ALU.mult, r=["zs", ("ps", 3)], w=[dkey])
            for i in range(5):
                zc(255 + 410 * i, 410, vv[:, 410 * i:410 * (i + 1)], "vv")
            self.tt("dve", vv[:, 0:1], vv[:, 0:1], cfl[:, 0:1], ALU.mult, r=["vv", "cfl"], w=["vv"])
            self.tt("dve", vv[:, 2049:2050], vv[:, 2049:2050], cfl[:, 1:2], ALU.mult, r=["vv", "cfl"], w=["vv"])
            zc(2560, 256, vc[:, 1:257], "vc")

            def outp(src, s0, h0, n, dst, skey):
                pp = self.psf[2]

                def mm(e):
                    ins = None
                    for k in range(8):
                        ins = e.matmul(pp[:, 0:n], lhsT=wz[:, k, 256:384], rhs=hT[:, k, h0:h0 + n], start=(k == 0), stop=(k == 7))
                    return ins
                P.op("pe", mm, r=[("wz", 2)], w=[("ps", 4)])
                P.op("dve", lambda e, cc=cc: e.tensor_scalar_mul(out=yt[:, 0:n], in0=src[:, s0 + 1:s0 + 1 + n], scalar1=cwt[:, cc, 1:2]),
                     r=[skey, "cwt"], w=["yt"])
                self.stt_("dve", yt[:, 0:n], src[:, s0:s0 + n], cwt[:, cc, 0:1], yt[:, 0:n], ALU.mult, ALU.add, r=[skey, "cwt", "yt"], w=["yt"])
                self.stt_("dve", yt[:, 0:n], src[:, s0 + 2:s0 + 2 + n], cwt[:, cc, 2:3], yt[:, 0:n], ALU.mult, ALU.add, r=[skey, "cwt", "yt"], w=["yt"])
                self.tt("dve", dst, yt[:, 0:n], pp[:, 0:n], ALU.mult, r=["yt", ("ps", 4)], w=["ycT"])
            for i in range(4):
                outp(vv, 512 * i, 256 + 512 * i, 512, ycT[:, cc, 512 * i:512 * (i + 1)], "vv")
            outp(vc, 0, 2560, 256, ycT[:, cc, 2048:2304], "vc")

    def merge_phase(self, x1, x2, win, wbranch, wout, xkey="x1"):
        P = self.P
        R = self.R
        yT = [R[:, 56784:66000].rearrange("p (k t) -> p k t", k=4), R[:, 38352:47568].rearrange("p (k t) -> p k t", k=4),
              R[:, 47568:56784].rearrange("p (k t) -> p k t", k=4)]
        o = 0
        wbr = R[:, o:o + 12288].rearrange("p (b c) -> p b c", b=12); o += 12288
        hTp = R[:, o:o + 4096].rearrange("p (k t) -> p k t", k=8); o += 4096
        wgt = [R[:, o + i * 3072:o + (i + 1) * 3072].rearrange("p (k c) -> p k c", k=8) for i in range(2)]; o += 6144
        acc = self.Rf(o, 512); o += 1024
        sgm = [self.Rf(o + i * 1024, 512) for i in range(2)]; o += 2048
        mT = R[:, o:o + 4096].rearrange("p (f t) -> p f t", f=8); o += 4096
        wo = R[:, o:o + 8192].rearrange("p (f c) -> p f c", f=8); o += 8192
        assert o <= 38352
        P.barrier()
        for b in range(3):
            P.dma("pool", wbr[:, b * 4:(b + 1) * 4, :], wbranch[b].rearrange("(k p) c -> p k c", p=128), w=[("wbr", b)])
        P.dma("pool", wo, wout.rearrange("(f p) c -> p f c", p=128), w=["wo"])
        wc = 0
        ac = 0
        for (t0, ntl) in ((0, 4), (4, 4), (8, 4), (12, 4), (16, 2)):
            n = ntl * 128
            tk0 = t0 * 128
            slots = []
            for i in range(ntl):
                t = t0 + i
                v = 0 if t < NT_LAT else 1
                s = self.xload(x1[t * 128:(t + 1) * 128, :], dkey=(xkey, t))
                slots.append(s)
                self.modulate_T(s, v, hTp, i, i % 2)
            for fi in range(8):
                wb_ = wc % 2
                wc += 1
                for b in range(3):
                    P.dma("pool", wgt[wb_][:, :, b * 128:(b + 1) * 128],
                          win[:, COL_G + b * 1024 + fi * 128:COL_G + b * 1024 + (fi + 1) * 128].rearrange("(k p) c -> p k c", p=128),
                          w=[("wgt", wb_, b)])
                for b in range(3):
                    pA, pG = self.psf[(ac % 2) * 2], self.psf[(ac % 2) * 2 + 1]
                    bA = 2 + (ac % 2) * 2

                    def mmA(e, b=b, fi=fi, pA=pA, n=n, tk0=tk0):
                        ins = None
                        for kc in range(4):
                            ins = e.matmul(pA[:, 0:n], lhsT=wbr[:, b * 4 + kc, fi * 128:(fi + 1) * 128], rhs=yT[b][:, kc, tk0:tk0 + n],
                                           start=(kc == 0), stop=(kc == 3))
                        return ins
                    P.op("pe", mmA, r=[("wbr", b)], w=[("ps", bA)])

                    def mmG(e, b=b, wb_=wb_, pG=pG, n=n):
                        ins = None
                        for k in range(8):
                            ins = e.matmul(pG[:, 0:n], lhsT=wgt[wb_][:, k, b * 128:(b + 1) * 128], rhs=hTp[:, k, 0:n],
                                           start=(k == 0), stop=(k == 7))
                        return ins
                    P.op("pe", mmG, r=[("wgt", wb_, b)] + [("hT", i) for i in range(ntl)], w=[("ps", bA + 1)])
                    sg = sgm[ac % 2]
                    self.act(sg[:, 0:n], pG[:, 0:n], AF.Sigmoid, r=[("ps", bA + 1)], w=[("sgm", ac % 2)])
                    if b == 0:
                        self.tt("dve", acc[:, 0:n], sg[:, 0:n], pA[:, 0:n], ALU.mult, r=[("sgm", ac % 2), ("ps", bA)], w=["acc"])
                    else:
                        self.tt("dve", sg[:, 0:n], sg[:, 0:n], pA[:, 0:n], ALU.mult, r=[("sgm", ac % 2), ("ps", bA)], w=[("sgm", ac % 2)])
                        if b == 1:
                            self.tt("dve", acc[:, 0:n], acc[:, 0:n], sg[:, 0:n], ALU.add, r=[("sgm", ac % 2), "acc"], w=["acc"])
                        else:
                            self.tt("dve", mT[:, fi, 0:n], acc[:, 0:n], sg[:, 0:n], ALU.add, r=[("sgm", ac % 2), "acc"], w=[("mT", fi)])
                    ac += 1
            for i in range(ntl):
                t = t0 + i
                v = 0 if t < NT_LAT else 1

                def mmo(e, i=i):
                    ins = None
                    for dh in range(2):
                        for fi in range(8):
                            ins = e.matmul(self.psf[4 + dh][:, :], lhsT=mT[:, fi, i * 128:(i + 1) * 128], rhs=wo[:, fi, dh * 512:(dh + 1) * 512],
                                           start=(fi == 0), stop=(fi == 7))
                    return ins
                P.op("pe", mmo, r=["wo"] + [("mT", fi) for fi in range(8)], w=[("ps", 6), ("ps", 7)])
                self.ln_epilogue(slots[i], v, 2, [6, 7], i % 2)
                P.dma("sp", x2[t * 128:(t + 1) * 128, :], self.XB[:, slots[i], :], r=[("XB", slots[i])], w=[("x2", t)], is_out=True)


def _din(nc, name, shape, dt=F32):
    return nc.dram_tensor(name, list(shape), dt, kind="ExternalInput").ap()


def _dout(nc, name, shape, dt=F32):
    return nc.dram_tensor(name, list(shape), dt, kind="ExternalOutput").ap()


FFN_BLOCKS = [list(range(0, 6)), list(range(6, 12)), list(range(12, 18))]


def _s5_inputs(nc, sfx=""):
    return dict(s5lam=_din(nc, "s5lam" + sfx, [128, 3, 32]), s5b=_din(nc, "s5b" + sfx, [128, 2, 32, 16]),
                s5c=_din(nc, "s5c" + sfx, [128, 2, 32, 16]), s5exp=_din(nc, "s5exp" + sfx, [128, NEXP]),
                s5d=_din(nc, "s5d" + sfx, [128, 32]), maskLU=_din(nc, "maskLU" + sfx, [128, 2, 128]))


def _decl_A(nc, sfx=""):
    d = dict(cvec=_din(nc, "cvec" + sfx, [128, 8, 2]), wmod=_din(nc, "wmod" + sfx, [D, NMOD * D]),
             bmodT=_din(nc, "bmodT" + sfx, [128, 72]), lng=_din(nc, "lng" + sfx, [3, D]), lnb=_din(nc, "lnb" + sfx, [3, D]),
             wg=_din(nc, "wg" + sfx, [D, DFF]), wu=_din(nc, "wu" + sfx, [D, DFF]), wd=_din(nc, "wd" + sfx, [DFF, D]),
             win=_din(nc, "win" + sfx, [D, DIN]))
    d.update(_s5_inputs(nc, sfx))
    return d


def _emit_A(C, P, d, xa, x1, modscr, eout, stout, utout, tabout, xkey_in=None):
    C.mod_phase(d["cvec"], d["wmod"], d["bmodT"], modscr)
    P.barrier()
    C.load_mod(modscr, (0, 1, 2))
    C.load_ln(d["lng"], d["lnb"], 0)
    C.ffn_phase(d["wg"], d["wu"], d["wd"], xa, x1, FFN_BLOCKS, xkey_in=xkey_in, xkey_out="x1")
    P.barrier()
    C.load_mod(modscr, (3, 4, 5))
    C.s5_phase(x1, d["win"], d["s5lam"], d["s5b"], d["s5c"], d["s5exp"], d["s5d"], d["maskLU"], None, "A", eout=eout,
               stio=stout, utio=utout, tabio=tabout)


def _decl_B(nc):
    d = dict(xh=_din(nc, "xh", [512, D]), modscr=_din(nc, "modscr", [72, 2, 128]), lng=_din(nc, "lng", [3, D]),
             lnb=_din(nc, "lnb", [3, D]), win=_din(nc, "win", [D, DIN]), wglu=_din(nc, "wglu", [512, 512]),
             elist=_din(nc, "elist", [128, 3, 2, 32]), aflag=_din(nc, "aflag", [128, 3]),
             biasT=_din(nc, "biasT", [4, 128, 2, 7, 128]), maskT=_din(nc, "maskT", [128, 27, 128]),
             cw=_din(nc, "cw", [128, 4, 3]), cflag=_din(nc, "cflag", [128, 2]), wbranch=_din(nc, "wbranch", [3, 512, D]),
             wout=_din(nc, "wout", [D, D]), wg=_din(nc, "wg", [D, DFF]), wu=_din(nc, "wu", [D, DFF]), wd=_din(nc, "wd", [DFF, D]),
             stin=_din(nc, "stin", [2, 128, 9344]), utin=_din(nc, "utin", [128, 9216]), tabin=_din(nc, "tabin", [128, 6144]))
    d.update(_s5_inputs(nc))
    return d


def _emit_B(C, P, d, x1, x2, x3):
    modscr = d["modscr"]
    C.load_mod(modscr, (3, 4, 5))
    C.load_ln(d["lng"], d["lnb"], 1)
    C.s5_phase(x1, d["win"], d["s5lam"], d["s5b"], d["s5c"], d["s5exp"], d["s5d"], d["maskLU"], d["wglu"], "B",
               elist=d["elist"], aflag=d["aflag"], xkey=None, stio=d["stin"], utio=d["utin"], tabio=d["tabin"])
    hT = C.build_hT_full(x1, d["xh"], xkey=None)
    C.attn_phase(hT, d["win"], d["biasT"], d["maskT"])
    C.conv_phase(hT, d["win"], d["cw"], d["cflag"])
    C.merge_phase(x1, x2, d["win"], d["wbranch"], d["wout"], xkey=None)
    P.barrier()
    C.load_mod(modscr, (6, 7, 8))
    C.load_ln(d["lng"], d["lnb"], 2)
    C.ffn_phase(d["wg"], d["wu"], d["wd"], x2, x3, FFN_BLOCKS, xkey_in="x2", xkey_out="x3")


def build_A():
    nc = bass.Bass("TRN2", target_bir_lowering=False)
    xa = _din(nc, "xa", [NT * 128, D])
    idn = _din(nc, "idn", [128, 128])
    d = _decl_A(nc)
    x1 = _dout(nc, "x1", [NT * 128, D])
    modscr = _dout(nc, "modscr", [72, 2, 128])
    eout = _dout(nc, "eout", [2, 128, 2, 16])
    stout = _dout(nc, "stout", [2, 128, 9344])
    utout = _dout(nc, "utout", [128, 9216])
    tabout = _dout(nc, "tabout", [128, 6144])
    P = Prog(nc)
    C = Core(nc, P)
    C.load_consts(idn)
    _emit_A(C, P, d, xa, x1, modscr, eout, stout, utout, tabout)
    P.emit()
    return nc


def build_B(with_next=False, dbg=False):
    nc = bass.Bass("TRN2", target_bir_lowering=False)
    x1 = _din(nc, "x1", [NT * 128, D])
    idn = _din(nc, "idn", [128, 128])
    d = _decl_B(nc)
    x2 = nc.dram_tensor("x2", [NT * 128, D], F32).ap()
    if with_next:
        dn = _decl_A(nc, "_n")
        x3 = nc.dram_tensor("x3", [NT * 128, D], F32).ap()
        x1n = _dout(nc, "x1_n", [NT * 128, D])
        modscr_n = _dout(nc, "modscr_n", [72, 2, 128])
        eout_n = _dout(nc, "eout_n", [2, 128, 2, 16])
        stout_n = _dout(nc, "stout_n", [2, 128, 9344])
        utout_n = _dout(nc, "utout_n", [128, 9216])
        tabout_n = _dout(nc, "tabout_n", [128, 6144])
    else:
        x3 = _dout(nc, "x3", [NT * 128, D])
    P = Prog(nc)
    C = Core(nc, P)
    C.load_consts(idn)
    _emit_B(C, P, d, x1, x2, x3)
    if with_next:
        P.barrier()
        _emit_A(C, P, dn, x3, x1n, modscr_n, eout_n, stout_n, utout_n, tabout_n, xkey_in=None)
    P.emit()
    return nc


def _s5_host(inp, l):
    G, Pn, H = 32, 64, 16
    lam = np.stack([inp["s5_lam_re"][l], inp["s5_lam_im"][l],
                    np.broadcast_to(inp["s5_log_dt"][l][:, :, None], (2, G, Pn))], 0)
    s5lam = np.ascontiguousarray(lam.transpose(1, 3, 0, 2).reshape(128, 3, G))
    bb = np.stack([inp["s5_b_re"][l], inp["s5_b_im"][l]], 0)
    s5b = np.ascontiguousarray(bb.transpose(1, 3, 0, 2, 4).reshape(128, 2, G, H))
    cc = np.stack([inp["s5_c_re"][l], inp["s5_c_im"][l]], 0)
    s5c = np.ascontiguousarray(cc.transpose(1, 4, 0, 2, 3).reshape(128, 2, G, H))
    s = np.arange(8, dtype=np.float32)
    blk = 8.0 * (np.arange(16, dtype=np.float32) + 1.0)
    ef = np.concatenate([7 - s, s + 1, -(s + 1), [8.0], [2048.0], blk])
    eb = np.concatenate([s, 8 - s, s - 8, [8.0], [2048.0], blk])
    s5exp = np.concatenate([np.tile(ef, (64, 1)), np.tile(eb, (64, 1))], 0).astype(np.float32)
    d = inp["s5_d"][l].reshape(G, H)
    s5d = np.ascontiguousarray(np.tile(d.T[None], (8, 1, 1)).reshape(128, G))
    sidx = np.repeat(np.arange(8), 16)
    mL = (sidx[None, :] >= sidx[:, None]).astype(np.float32)
    mU = (sidx[None, :] <= sidx[:, None]).astype(np.float32)
    maskLU = np.ascontiguousarray(np.stack([mL, mU], 1))
    return dict(s5lam=s5lam.astype(np.float32), s5b=s5b.astype(np.float32), s5c=s5c.astype(np.float32), s5exp=s5exp,
                s5d=s5d.astype(np.float32), maskLU=maskLU)


def _bias_host(rpb):
    a = np.arange(2)[:, None, None, None]
    kc = np.arange(64)[None, :, None, None]
    b = np.arange(2)[None, None, :, None]
    qc = np.arange(64)[None, None, None, :]
    dc = np.clip(kc - qc + 15, 0, 30) + 0 * a + 0 * b
    out = np.zeros((4, 128, 2, 7, 128), np.float32)
    for ji, jo in enumerate(range(-1, 6)):
        dr = np.clip(2 * jo + a - b + 3, 0, 14) + 0 * kc + 0 * qc
        for h in range(8):
            out[h // 2, :, h % 2, ji, :] = rpb[h][dr, dc].reshape(128, 128)
    return out


def _mask_host(q):
    classes = [(0, list(range(0, 6))), (1, list(range(0, 5))), (2, list(range(0, 5))), (14, list(range(0, 5))), (15, list(range(-1, 5)))]
    out = np.zeros((128, 27, 128), np.float32)
    kc = np.arange(64)[:, None]
    qc = np.arange(64)[None, :]
    cs = np.clip(qc - 8, 0, 48)
    colv = (kc >= cs) & (kc < cs + 16)
    idx = 0
    for (m, jos) in classes:
        for jo in jos:
            for a in range(2):
                for b in range(2):
                    r = 32 * q + 2 * m + b
                    kr = 32 * q - 4 + 2 * (m + jo) + a
                    rs = min(max(r - 4, 0), 120)
                    ok = (rs <= kr < rs + 8)
                    v = colv if ok else np.zeros_like(colv)
                    out[a * 64:(a + 1) * 64, idx, b * 64:(b + 1) * 64] = np.where(v, 0.0, -30000.0)
            idx += 1
    return out


def _common_host(inp, l):
    d = _s5_host(inp, l)
    d["idn"] = np.eye(128, dtype=np.float32)
    d["lng"] = np.ascontiguousarray(inp["ln_g"][l])
    d["lnb"] = np.ascontiguousarray(inp["ln_b"][l])
    d["win"] = np.ascontiguousarray(inp["w_in"][l])
    return d


def _mapsA(inp, l, x, com, sfx=""):
    maps = []
    for c in range(8):
        b = c // 4
        cv = np.stack([inp["c"][b], inp["c_ctx"]], -1)
        m = {k + sfx: v for k, v in com.items() if k != "idn"}
        m.update({"cvec" + sfx: np.ascontiguousarray(cv.reshape(8, 128, 2).transpose(1, 0, 2)), "wmod" + sfx: inp["w_mod"][l],
                  "bmodT" + sfx: np.ascontiguousarray(inp["b_mod"][l].reshape(72, 128).T),
                  "wg" + sfx: inp["ffn_wg"][l, 0], "wu" + sfx: inp["ffn_wu"][l, 0], "wd" + sfx: inp["ffn_wd"][l, 0]})
        if x is not None:
            m["xa"] = np.ascontiguousarray(x[c])
        maps.append(m)
    return maps


def _mapsB(inp, l, com, x1s, mods, eouts, sts, uts, tabs):
    biasT = _bias_host(inp["na_rpb"][l])
    cwh = np.ascontiguousarray(inp["conv_w"][l].reshape(3, 4, 128).transpose(2, 1, 0))
    maps = []
    for c in range(8):
        b, q = c // 4, c % 4
        E = [eouts[b * 4 + qq].transpose(1, 2, 0, 3).reshape(128, 2, 32) for qq in range(4)]
        xh = np.zeros((512, D), np.float32)
        if q > 0:
            xh[0:256] = x1s[c - 1][2048 - 256:2048]
        if q < 3:
            xh[256:512] = x1s[c + 1][0:256]
        el = np.zeros((128, 3, 2, 32), np.float32)
        af = np.zeros((128, 3), np.float32)
        for s in range(3):
            qf = q - 3 + s
            if qf >= 0:
                el[0:64, s] = E[qf][0:64]
                af[0:64, s] = 1.0
            qb = q + 3 - s
            if qb <= 3:
                el[64:128, s] = E[qb][64:128]
                af[64:128, s] = 1.0
        cfl = np.zeros((128, 2), np.float32)
        cfl[:, 0] = 1.0 if q > 0 else 0.0
        cfl[:, 1] = 1.0 if q < 3 else 0.0
        m = dict(com)
        m.update(x1=x1s[c], xh=xh, modscr=mods[c], stin=sts[c], utin=uts[c], tabin=tabs[c], wglu=np.ascontiguousarray(inp["s5_w_glu"][l]), elist=el, aflag=af,
                 biasT=biasT, maskT=_mask_host(q), cw=cwh, cflag=cfl, wbranch=np.ascontiguousarray(inp["w_branch"][l]),
                 wout=np.ascontiguousarray(inp["w_out"][l]), wg=inp["ffn_wg"][l, 1], wu=inp["ffn_wu"][l, 1], wd=inp["ffn_wd"][l, 1])
        maps.append(m)
    return maps


def kernel(**inp):
    inp = {k: np.asarray(v) for k, v in inp.items()}
    ncore = 8
    cores = list(range(ncore))
    x = [np.concatenate([inp["x"][c // 4, (c % 4) * 2048:(c % 4 + 1) * 2048], inp["ctx"][c // 4]], 0) for c in range(ncore)]
    com0 = _common_host(inp, 0)
    com1 = _common_host(inp, 1)
    m1 = _mapsA(inp, 0, x, com0)
    for m in m1:
        m["idn"] = com0["idn"]
    r1 = run_bass_kernel_spmd(build_A(), m1, core_ids=cores).results
    m2 = _mapsB(inp, 0, com0, [r["x1"] for r in r1], [r["modscr"] for r in r1], [r["eout"] for r in r1],
                [r["stout"] for r in r1], [r["utout"] for r in r1], [r["tabout"] for r in r1])
    mn = _mapsA(inp, 1, None, com1, "_n")
    for a, b_ in zip(m2, mn):
        a.update(b_)
    r2 = run_bass_kernel_spmd(build_B(with_next=True), m2, core_ids=cores).results
    m3 = _mapsB(inp, 1, com1, [r["x1_n"] for r in r2], [r["modscr_n"] for r in r2], [r["eout_n"] for r in r2],
                [r["stout_n"] for r in r2], [r["utout_n"] for r in r2], [r["tabout_n"] for r in r2])
    r3 = run_bass_kernel_spmd(build_B(with_next=False), m3, core_ids=cores).results
    out = np.zeros((2, 8192, D), np.float32)
    for c in range(ncore):
        out[c // 4, (c % 4) * 2048:(c % 4 + 1) * 2048] = r3[c]["x3"][0:2048]
    return out
```

```python
import contextlib
from concourse.bass_utils import run_bass_kernel_spmd
import numpy as np
import concourse.bass as bass
import concourse.mybir as mybir

F32 = mybir.dt.float32
BF16 = mybir.dt.bfloat16
ALU = mybir.AluOpType
AF = mybir.ActivationFunctionType
AX = mybir.AxisListType

ENGS = ["pe", "act", "dve", "pool", "sp"]
NDMA = 12


class Op:
    __slots__ = ("eng", "fn", "waits", "sem", "inc", "final")

    def __init__(self, eng, fn):
        self.eng = eng
        self.fn = fn
        self.waits = []
        self.sem = None
        self.inc = 0
        self.final = False


class Prog:
    def __init__(self, nc):
        self.nc = nc
        self.ops = []
        self.stack = contextlib.ExitStack()
        self.cnt = {e: 0 for e in ENGS}
        self.known = {e: {} for e in ENGS}
        self.lastw = {}
        self.readers = {}
        self.semh = {}
        for e in ["pe", "act", "dve", "pool"]:
            self.semh[e] = self.stack.enter_context(nc.semaphore("c_" + e))
        self.dma_tot = {}
        self.dma_rr = {}
        for q in ["sp", "pool", "act"]:
            for i in range(NDMA):
                k = ("dma", q, i)
                self.semh[k] = self.stack.enter_context(nc.semaphore("d_%s%d" % (q, i)))
                self.dma_tot[k] = 0
            self.dma_rr[q] = 0
        self.out_dmas = []

    def sb(self, name, shape, dt):
        return self.stack.enter_context(self.nc.sbuf_tensor(name, list(shape), dt))

    def ps(self, name, shape, dt):
        return self.stack.enter_context(self.nc.psum_tensor(name, list(shape), dt))

    def _need(self, op, eng, semkey, val):
        if semkey == "pe" and eng == "pe":
            return
        if self.known[eng].get(semkey, 0) >= val:
            return
        self.known[eng][semkey] = val
        op.waits.append((semkey, val))

    def op(self, eng, fn, r=(), w=(), dma=False, is_out=False):
        o = Op(eng, fn)
        for k in r:
            lw = self.lastw.get(k)
            if lw is not None:
                self._need(o, eng, lw[0], lw[1])
        for k in w:
            lw = self.lastw.get(k)
            if lw is not None:
                self._need(o, eng, lw[0], lw[1])
            for rd in self.readers.get(k, ()):
                self._need(o, eng, rd[0], rd[1])
        if dma:
            q = eng
            i = self.dma_rr[q]
            self.dma_rr[q] = (i + 1) % NDMA
            sk = ("dma", q, i)
            if self.dma_tot[sk] > 0:
                self._need(o, eng, sk, self.dma_tot[sk])
            self.dma_tot[sk] += 16
            o.sem, o.inc = sk, 16
            done = (sk, self.dma_tot[sk])
            if is_out:
                self.out_dmas.append(done)
        else:
            self.cnt[eng] += 1
            o.sem, o.inc = eng, 1
            done = (eng, self.cnt[eng])
        m = {}
        for sk, v in o.waits:
            m[sk] = max(m.get(sk, 0), v)
        o.waits = list(m.items())
        for k in r:
            self.readers.setdefault(k, []).append(done)
        for k in w:
            self.lastw[k] = done
            self.readers[k] = []
        self.ops.append(o)
        return o

    def barrier(self):
        allk = [(e, self.cnt[e]) for e in ["pe", "act", "dve", "pool"] if self.cnt[e] > 0]
        allk += [(k, v) for k, v in self.dma_tot.items() if v > 0]
        for eng in ENGS:
            o = Op(eng, None)
            for sk, v in allk:
                if sk == eng and eng == "pe":
                    continue
                if self.known[eng].get(sk, 0) < v:
                    self.known[eng][sk] = v
                    o.waits.append((sk, v))
            self.ops.append(o)
        self.lastw = {}
        self.readers = {}

    def dma(self, q, out, in_, r=(), w=(), is_out=False):
        return self.op(q, lambda e: e.dma_start(out=out, in_=in_), r=r, w=w, dma=True, is_out=is_out)

    def emit(self):
        nc = self.nc
        fin = list(self.out_dmas)
        with nc.Block() as block:
            for eng in ENGS:
                ops = [o for o in self.ops if o.eng == eng]

                def body(e, ops=ops, eng=eng):
                    for o in ops:
                        for sk, v in o.waits:
                            e.wait_ge(self.semh[sk], v)
                        if o.fn is None:
                            continue
                        ins = o.fn(e)
                        ins.then_inc(self.semh[o.sem], o.inc)
                    if eng == "sp":
                        m = {}
                        for sk, v in fin:
                            m[sk] = max(m.get(sk, 0), v)
                        for sk, v in self.dma_tot.items():
                            if v > 0:
                                m[sk] = max(m.get(sk, 0), v)
                        for sk, v in m.items():
                            e.wait_ge(self.semh[sk], v)
                        for ce in ["pe", "act", "dve", "pool"]:
                            if self.cnt[ce] > 0:
                                e.wait_ge(self.semh[ce], self.cnt[ce])

                name = {"pe": "tensor", "act": "scalar", "dve": "vector", "pool": "gpsimd", "sp": "sync"}[eng]
                getattr(block, name)(body)
        self.stack.close()


D = 1024
DFF = 2816
NJ = DFF // 128
NMOD = 9
DIN = 6656
COL_U, COL_K, COL_V, COL_Q, COL_Z, COL_B, COL_C, COL_G = 0, 512, 1024, 1536, 2048, 2560, 3072, 3584
ALPHA = (2.0 * 2) ** 0.25
EPS = 1e-5
NT_LAT = 16
NT_CTX = 2
NT = NT_LAT + NT_CTX
NCH = 288
STW = 292
NEXP = 42
TWO_PI = 6.283185307179586
CW1 = 6.28125
CW2 = TWO_PI - 6.28125
RN = 66000


class StopBuild(Exception):
    pass


class Core:
    def __init__(self, nc, P):
        self.nc = nc
        self.P = P
        sb = P.sb
        self.modbc = sb("modbc", [128, 6, D], F32)
        self.lnp = sb("lnp", [128, 2, D], F32)
        self.idb = sb("idb", [128, 128], BF16)
        self.idf = sb("idf", [128, 128], F32)
        self.XB = sb("XB", [128, 6, D], F32)
        self.hb = sb("hb", [128, 2, D], BF16)
        self.tmpf = sb("tmpf", [128, 2, D], F32)
        self.zt = sb("zt", [128, 2, D], F32)
        self.stt = sb("stt", [128, 2, 2, 6], F32)
        self.mv = sb("mv", [128, 2, 8], F32)
        self.R = sb("R", [128, RN], BF16)
        self.psb = [P.ps("psb%d" % i, [128, 1024], BF16) for i in range(2)]
        self.psf = [P.ps("psf%d" % i, [128, 512], F32) for i in range(6)]
        self.xslot = 0

    def chk(self, n):
        if getattr(self, 's5stop', None) == n:
            raise StopBuild()

    def tt(self, eng, out, in0, in1, op, r, w):
        return self.P.op(eng, lambda e: e.tensor_tensor(out=out, in0=in0, in1=in1, op=op), r=r, w=w)

    def ts(self, eng, out, in0, s1, s2, op0, op1, r, w):
        return self.P.op(eng, lambda e: e.tensor_scalar(out=out, in0=in0, scalar1=s1, scalar2=s2, op0=op0, op1=op1), r=r, w=w)

    def stt_(self, eng, out, in0, scalar, in1, op0, op1, r, w):
        return self.P.op(eng, lambda e: e.scalar_tensor_tensor(out=out, in0=in0, scalar=scalar, in1=in1, op0=op0, op1=op1), r=r, w=w)

    def act(self, out, in_, func, r, w, scale=1.0, bias=0.0):
        return self.P.op("act", lambda e: e.activation(out=out, in_=in_, func=func, bias=bias, scale=scale), r=r, w=w)

    def cp(self, eng, out, in_, r, w):
        if eng == "act":
            return self.act(out, in_, AF.Copy, r, w)
        return self.P.op(eng, lambda e: e.tensor_copy(out=out, in_=in_), r=r, w=w)

    def Rf(self, off_bf16, n_f32):
        return self.R[:, off_bf16:off_bf16 + 2 * n_f32].bitcast(F32)

    def load_consts(self, idn):
        P = self.P
        P.dma("pool", self.idb[:], idn, w=["idb"])
        P.dma("sp", self.idf[:], idn, w=["idf"])

    def xload(self, src_rows, dkey=None):
        s = self.xslot
        self.xslot = (s + 1) % 6
        q = "sp" if s % 2 == 0 else "act"
        self.P.dma(q, self.XB[:, s, :], src_rows, r=[dkey] if dkey else [], w=[("XB", s)])
        return s

    def mod_phase(self, cvec, wmod, bmodT, modscr, bmod_row=None):
        P = self.P
        NWB = 3
        B0 = NWB * 4096
        Rf = self.Rf(0, B0 + 64 + 18432)
        wm = [Rf[:, i * 4096:(i + 1) * 4096].rearrange("p (k c) -> p k c", k=8) for i in range(NWB)]
        sc = Rf[:, B0:B0 + 16].rearrange("p (k v) -> p k v", k=8)
        mrow = Rf[0:2, B0 + 64:B0 + 64 + 9216]
        brow = Rf[0:2, B0 + 64 + 9216:B0 + 64 + 18432]
        P.dma("sp", sc, cvec, w=["sc"])
        P.dma("act", brow, bmod_row.to_broadcast([2, NMOD * D]), w=["brow"])
        self.act(sc, sc, AF.Silu, r=["sc"], w=["sc"])
        for cc in range(18):
            b = cc % NWB
            q = "sp" if cc % 2 == 0 else "act"
            P.dma(q, wm[b], wmod[:, cc * 512:(cc + 1) * 512].rearrange("(k p) c -> p k c", p=128), w=[("wm", b)])
            pp = self.psf[cc % 2]

            def mm(e, b=b, pp=pp):
                ins = None
                for k in range(8):
                    ins = e.matmul(pp[0:2, :], lhsT=sc[:, k, :], rhs=wm[b][:, k, :], start=(k == 0), stop=(k == 7))
                return ins
            P.op("pe", mm, r=[("wm", b), "sc"], w=[("ps", 2 + cc % 2)])
            self.tt("dve", mrow[:, cc * 512:(cc + 1) * 512], pp[0:2, :], brow[:, cc * 512:(cc + 1) * 512], ALU.add,
                    r=[("ps", 2 + cc % 2), "brow"], w=["mrow"])
        for idx in (1, 4, 7):
            sl = mrow[:, idx * D:(idx + 1) * D]
            P.op("dve", lambda e, sl=sl: e.tensor_scalar_add(out=sl, in0=sl, scalar1=1.0), r=["mrow"], w=["mrow"])
        for idx in (2, 8):
            sl = mrow[:, idx * D:(idx + 1) * D]
            P.op("dve", lambda e, sl=sl: e.tensor_scalar_mul(out=sl, in0=sl, scalar1=0.5), r=["mrow"], w=["mrow"])
        P.dma("sp", modscr.rearrange("j v f -> v j f"), mrow.rearrange("v (j f) -> v j f", f=128), r=["mrow"], w=["modscr"], is_out=True)

    def load_mod(self, modscr, idxs, q="sp"):
        P = self.P
        for v in range(2):
            for i, idx in enumerate(idxs):
                dst = self.modbc[:, v * 3 + i, :].rearrange("p (j f) -> p j f", j=8)
                src = modscr[idx * 8:(idx + 1) * 8, v:v + 1, :].rearrange("j v f -> v j f").to_broadcast([128, 8, 128])
                P.dma(q, dst, src, r=["modscr"], w=[("modbc", v * 3 + i)])

    def load_ln(self, lng, lnb, s, q="act"):
        P = self.P
        P.dma(q, self.lnp[:, 0, :], lng[s:s + 1, :].to_broadcast([128, D]), w=[("lnp", 0)])
        P.dma(q, self.lnp[:, 1, :], lnb[s:s + 1, :].to_broadcast([128, D]), w=[("lnp", 1)])

    def modulate_T(self, xs, v, hT, pos, slot):
        P = self.P
        tm = self.tmpf[:, slot, :]
        hb = self.hb[:, slot, :]
        X, xk = (self.XB[:, xs, :], ("XB", xs)) if isinstance(xs, int) else xs
        self.tt("dve", tm, X, self.modbc[:, v * 3 + 1, :], ALU.mult, r=[xk, ("modbc", v * 3 + 1)], w=[("tmpf", slot)])
        self.tt("dve", hb, tm, self.modbc[:, v * 3 + 0, :], ALU.add, r=[("tmpf", slot), ("modbc", v * 3 + 0)], w=[("hb", slot)])
        pb = self.psb[slot]

        def tr(e):
            ins = None
            for k in range(8):
                ins = e.transpose(pb[:, k * 128:(k + 1) * 128], hb[:, k * 128:(k + 1) * 128], self.idb[:])
            return ins
        P.op("pe", tr, r=[("hb", slot), "idb"], w=[("ps", slot)])
        self.act(hT[:, :, pos * 128:(pos + 1) * 128], pb[:, :].rearrange("p (k c) -> p k c", k=8), AF.Copy,
                 r=[("ps", slot)], w=[("hT", pos)])

    def zbuf(self, i):
        i = i % 4
        if i < 2:
            return self.zt[:, i, :], ("zt", i)
        return self.tmpf[:, i - 2, :], ("tmpf", i - 2)

    def ln_evac(self, v, gate_i, ps_banks, zi):
        z, zk = self.zbuf(zi)
        gt = self.modbc[:, v * 3 + gate_i, :]
        for dh in range(2):
            b = ps_banks[dh]
            self.tt("dve", z[:, dh * 512:(dh + 1) * 512], self.psf[b - 2][:, :], gt[:, dh * 512:(dh + 1) * 512], ALU.mult,
                    r=[("ps", b), ("modbc", v * 3 + gate_i)], w=[zk])

    def ln_rest(self, xs, zi, slot):
        P = self.P
        z, zk = self.zbuf(zi)
        st = self.stt[:, slot]
        mv = self.mv[:, slot, :]
        X, xk = (self.XB[:, xs, :], ("XB", xs)) if isinstance(xs, int) else xs
        self.stt_("dve", z, X, ALPHA, z, ALU.mult, ALU.add, r=[xk, zk], w=[zk])
        for c in range(2):
            P.op("dve", lambda e, c=c: e.bn_stats(out=st[:, c, :], in_=z[:, c * 512:(c + 1) * 512]),
                 r=[zk], w=[("stt", slot, c)])
        P.op("dve", lambda e: e.bn_aggr(out=mv[:, 0:2], in_=st), r=[("stt", slot, 0), ("stt", slot, 1)], w=[("mv", slot)])
        P.op("dve", lambda e: e.tensor_scalar_add(out=mv[:, 4:5], in0=mv[:, 1:2], scalar1=EPS), r=[("mv", slot)], w=[("mv", slot)])
        self.act(mv[:, 5:6], mv[:, 4:5], AF.Sqrt, r=[("mv", slot)], w=[("mv", slot)])
        P.op("dve", lambda e: e.reciprocal(out=mv[:, 2:3], in_=mv[:, 5:6]), r=[("mv", slot)], w=[("mv", slot)])
        self.stt_("dve", mv[:, 3:4], mv[:, 0:1], -1.0, mv[:, 2:3], ALU.mult, ALU.mult, r=[("mv", slot)], w=[("mv", slot)])
        self.act(z, z, AF.Identity, r=[zk, ("mv", slot)], w=[zk], scale=mv[:, 2:3], bias=mv[:, 3:4])
        self.tt("pool", z, z, self.lnp[:, 0, :], ALU.mult, r=[zk, ("lnp", 0)], w=[zk])
        self.tt("pool", X, z, self.lnp[:, 1, :], ALU.add, r=[zk, ("lnp", 1)], w=[xk])

    def ln_epilogue(self, xs, v, gate_i, ps_banks, slot):
        self.ln_evac(v, gate_i, ps_banks, slot)
        self.ln_rest(xs, slot, slot)

    def ffn_phase(self, wg, wu, wd, xin, xout, blocks, xkey_in=None, xkey_out="xout"):
        P = self.P
        R = self.R
        TB = 768
        hT = R[:, 0:8 * TB].rearrange("p (k t) -> p k t", k=8)
        o = 8 * TB
        actT = R[:, o:o + NJ * TB].rearrange("p (j t) -> p j t", j=NJ)
        o += NJ * TB
        wgu = [[R[:, o + (b * 2 + i) * 2048: o + (b * 2 + i + 1) * 2048].rearrange("p (k c) -> p k c", k=8)
                for i in range(2)] for b in range(2)]
        o += 4 * 2048
        wdr = R[:, o:o + NJ * 1024].rearrange("p (j c) -> p j c", j=NJ)
        o += NJ * 1024
        wds = [self.Rf(o + b * 4096, 2048).rearrange("p (j c) -> p j c", j=2) for b in range(2)]
        o += 2 * 4096
        sgs = [R[:, o + b * 768: o + (b + 1) * 768].bitcast(F32) for b in range(2)]
        o += 2 * 768
        assert o <= RN
        wgc = 0
        zrot = 0
        for bi, blk in enumerate(blocks):
            nb = len(blk)
            for pos, t in enumerate(blk):
                v = 0 if t < NT_LAT else 1
                s = self.xload(xin[t * 128:(t + 1) * 128, :], dkey=(xkey_in, t) if xkey_in else None)
                self.modulate_T(s, v, hT, pos, pos % 2)
            groups = [(g0, min(g0 + 3, nb)) for g0 in range(0, nb, 3)]
            for jj in range(NJ // 2):
                b = wgc % 2
                wgc += 1
                for i, wsrc in enumerate((wg, wu)):
                    P.dma("pool", wgu[b][i], wsrc[:, jj * 256:(jj + 1) * 256].rearrange("(k p) c -> p k c", p=128),
                          w=[("wgu", b, i)])
                if bi == 0:
                    P.dma("sp" if jj % 2 == 0 else "act", wds[jj % 2], wd[jj * 256:(jj + 1) * 256, :].rearrange("(j p) c -> p j c", p=128),
                          w=[("wds", jj % 2)])
                    self.cp("act", wdr[:, 2 * jj:2 * jj + 2, :], wds[jj % 2], r=[("wds", jj % 2)], w=[("wdr", jj)])
                for jl in range(2):
                    j = jj * 2 + jl
                    for gi, (g0, g1) in enumerate(groups):
                        n = (g1 - g0) * 128
                        pg = self.psf[(gi % 2) * 2]
                        pu = self.psf[(gi % 2) * 2 + 1]
                        bg = 2 + (gi % 2) * 2
                        for i, pp in enumerate((pg, pu)):
                            def mm(e, i=i, pp=pp, b=b, jl=jl, g0=g0, n=n):
                                ins = None
                                for k in range(8):
                                    ins = e.matmul(pp[:, 0:n], lhsT=wgu[b][i][:, k, jl * 128:(jl + 1) * 128],
                                                   rhs=hT[:, k, g0 * 128:g0 * 128 + n], start=(k == 0), stop=(k == 7))
                                return ins
                            P.op("pe", mm, r=[("wgu", b, i)] + [("hT", p_) for p_ in range(g0, g1)], w=[("ps", bg + i)])
                        sg = sgs[gi % 2]
                        self.act(sg[:, 0:n], pg[:, 0:n], AF.Silu, r=[("ps", bg)], w=[("sgs", gi % 2)])
                        self.tt("dve", actT[:, j, g0 * 128:g0 * 128 + n], sg[:, 0:n], pu[:, 0:n], ALU.mult,
                                r=[("sgs", gi % 2), ("ps", bg + 1)], w=[("act", j, gi)])
            for (g0, g1) in groups:
                gi = g0 // 3
                for jj in range(NJ // 2):
                    def mm(e, jj=jj, g0=g0, g1=g1):
                        ins = None
                        for jl in range(2):
                            j = jj * 2 + jl
                            for p_ in range(g0, g1):
                                for dh in range(2):
                                    ins = e.matmul(self.psf[(p_ - g0) * 2 + dh][:, :], lhsT=actT[:, j, p_ * 128:(p_ + 1) * 128],
                                                   rhs=wdr[:, j, dh * 512:(dh + 1) * 512],
                                                   start=(j == 0), stop=(j == NJ - 1))
                        return ins
                    P.op("pe", mm, r=[("wdr", jj)] + [("act", jj * 2 + jl, gi) for jl in range(2)],
                         w=[("ps", 2 + (p_ - g0) * 2 + dh) for p_ in range(g0, g1) for dh in range(2)])
                xs_ = {}
                for p_ in range(g0, g1):
                    t = blk[p_]
                    v = 0 if t < NT_LAT else 1
                    bks = [2 + (p_ - g0) * 2, 3 + (p_ - g0) * 2]
                    xs_[p_] = self.xload(xin[t * 128:(t + 1) * 128, :], dkey=(xkey_in, t) if xkey_in else None)
                    self.ln_evac(v, 2, bks, zrot + (p_ - g0))
                for p_ in range(g0, g1):
                    t = blk[p_]
                    self.ln_rest(xs_[p_], zrot + (p_ - g0), p_ % 2)
                    P.dma("sp", xout[t * 128:(t + 1) * 128, :], self.XB[:, xs_[p_], :], r=[("XB", xs_[p_])],
                          w=[(xkey_out, t)], is_out=True)
                zrot += (g1 - g0)

    def make_hT(self, hT, entries):
        for i, (src, v, pos, dkey) in enumerate(entries):
            s = self.xload(src, dkey=dkey)
            self.modulate_T(s, v, hT, pos, i % 2)

    def s5_phase(self, x1, win, s5lam, s5b, s5c, s5exp, s5d, maskLU, wglu, mode, elist=None, aflag=None, eout=None,
                 xkey="x1", stio=None, utio=None, tabio=None):
        P = self.P
        R = self.R
        NG = 32
        UT8 = self.Rf(0, 9216).rearrange("p (g c) -> p g c", g=NG)
        STo = 18432
        ST = self.Rf(STo, 9344).rearrange("p (c g w) -> p c g w", c=2, g=16)
        SMo = 37120
        sm = self.Rf(SMo, 616)
        gT = R[:, 38352:47568].rearrange("p (k t) -> p k t", k=4)
        Gt32 = self.Rf(47568, 3072).rearrange("p (b t c) -> p b t c", b=3, t=8)
        Gb = R[:, 53712:56784].rearrange("p (b t c) -> p b t c", b=3, t=8)
        HBf = self.hb[:, :, :].rearrange("p a d -> p (a d)").bitcast(F32)
        yaT = R[:, 56784:66000].rearrange("p (k t) -> p k t", k=4)
        hTp = R[:, STo:STo + 8192].rearrange("p (k t) -> p k t", k=8)
        Ut = self.Rf(STo + 8192, 4096).rearrange("p (g j h) -> p g j h", g=32, j=8)
        wU = R[:, 38352:38352 + 4096].rearrange("p (k c) -> p k c", k=8)
        P.barrier()
        if mode == "A":
            P.dma("pool", wU, win[:, COL_U:COL_U + 512].rearrange("(k p) c -> p k c", p=128), w=["wU"])
            blkdefs = [(0, 8, 0, 128), (8, 8, 128, 128), (16, 2, 256, 32)]
            for (t0, ntl, cb, ncn) in blkdefs:
                ents = []
                for i in range(ntl):
                    t = t0 + i
                    ents.append((x1[t * 128:(t + 1) * 128, :], 0 if t < NT_LAT else 1, i, (xkey, t)))
                self.make_hT(hTp, ents)
                ntok = ntl * 128
                for j in range(8):
                    pp = self.psf[j % 2]

                    def mm(e, j=j, pp=pp, ncn=ncn, ntok=ntok):
                        ins = None
                        for k in range(8):
                            ins = e.matmul(pp[0:ncn, :], lhsT=hTp[:, k, j:ntok:8], rhs=wU[:, k, :], start=(k == 0), stop=(k == 7))
                        return ins
                    P.op("pe", mm, r=["wU"] + [("hT", i) for i in range(ntl)], w=[("ps", 2 + j % 2)])
                    self.cp("act" if j % 2 == 0 else "dve", Ut[0:ncn, :, j, :], pp[0:ncn, :].rearrange("p (g h) -> p g h", g=32),
                            r=[("ps", 2 + j % 2)], w=[("Ut", j)])
                for g0 in range(0, NG, 4):
                    pq = self.psf[2 + (g0 // 4) % 2]

                    def tr(e, g0=g0, pq=pq, ncn=ncn):
                        ins = None
                        for gg in range(4):
                            g = g0 + gg
                            ins = e.transpose(pq[:, gg * 128:gg * 128 + ncn], Ut[0:ncn, g].rearrange("p j h -> p (j h)"), self.idf[0:ncn, 0:ncn])
                        return ins
                    P.op("pe", tr, r=[("Ut", j) for j in range(8)] + ["idf"], w=[("ps", 4 + (g0 // 4) % 2)])
                    self.cp("act" if (g0 // 4) % 2 == 0 else "dve", UT8[:, g0:g0 + 4, cb:cb + ncn],
                            pq[:, :].rearrange("p (g c) -> p g c", g=4)[:, :, 0:ncn],
                            r=[("ps", 4 + (g0 // 4) % 2)], w=[("UT8", g0 // 4, cb)])
            P.barrier()
        if mode == "A":
            P.dma("sp", utio, self.Rf(0, 9216), is_out=True)
        else:
            P.dma("sp", self.Rf(0, 9216), utio, w=["UT8in"])
            P.barrier()
        XBf = self.XB[:, :, :].rearrange("p a d -> p (a d)")
        o = [0]

        def xa(n):
            a = XBf[:, o[0]:o[0] + n]
            o[0] += n
            return a
        NTB = NG * NEXP
        Tr = xa(NTB).rearrange("p (g e) -> p g e", g=NG)
        Ti = xa(NTB).rearrange("p (g e) -> p g e", g=NG)
        Bb = xa(1024).rearrange("p (c g h) -> p c g h", c=2, g=NG)
        Cc = xa(1024).rearrange("p (c g h) -> p c g h", c=2, g=NG)
        lam3 = xa(96).rearrange("p (a g) -> p a g", a=3)
        dtv, ar, ai = xa(32), xa(32), xa(32)
        exps = xa(NEXP + 2)
        qr, qi, den, t32a, t32b = xa(32), xa(32), xa(32), xa(32), xa(32)
        mLU = xa(256).rearrange("p (a c) -> p a c", a=2)
        Dg = xa(32)
        cst = xa(8)
        af = xa(4)
        El = sm[:, 226:418].rearrange("p (s c g) -> p s c g", s=3, c=2)
        y8off = o[0]
        Y8 = [xa(288), xa(288)]
        assert o[0] <= 6144
        STf = self.Rf(STo, 9344)
        argr_f = STf[:, 0:NTB]
        argi_f = STf[:, NTB:2 * NTB]
        argr = argr_f.rearrange("p (g e) -> p g e", g=NG)
        argi = argi_f.rearrange("p (g e) -> p g e", g=NG)
        mag = STf[:, 2 * NTB:3 * NTB]
        kf = STf[:, 0:NTB]
        ki = STf[:, 3 * NTB:4 * NTB].bitcast(mybir.dt.int32)
        rr = STf[:, 4 * NTB:5 * NTB]
        trg = STf[:, 5 * NTB:6 * NTB]
        btmp = STf[:, 6 * NTB:6 * NTB + 1024].rearrange("p (c g h) -> p c g h", c=2, g=NG)
        assert 6 * NTB + 1024 <= 9344
        if mode == "B":
            P.dma("sp", XBf[:, 0:6144], tabio, w=["tab"])
            P.dma("sp", af[:, 0:3], aflag, r=["tab"], w=["af"])
            P.dma("act", El, elist, w=["El"])
            P.barrier()
        else:
            P.dma("sp", lam3, s5lam, w=["lam3"])
            P.dma("sp", Bb, s5b, w=["Bb"])
            P.dma("act", Cc, s5c, w=["Cc"])
            P.dma("sp", exps[:, 0:NEXP], s5exp, w=["exps"])
            P.dma("act", mLU, maskLU, w=["mLU"])
            P.dma("sp", Dg, s5d, w=["Dg"])
            if mode == "B":
                P.dma("sp", af[:, 0:3], aflag, w=["af"])
                P.dma("act", El, elist, w=["El"])
            P.op("dve", lambda e: e.memset(cst[:, 0:1], -3.1415925), w=["cst"])
            self.act(dtv, lam3[:, 2, :], AF.Exp, r=["lam3"], w=["dtv"])
            self.tt("dve", ar, lam3[:, 0, :], dtv, ALU.mult, r=["lam3", "dtv"], w=["ar"])
            self.tt("dve", ai, lam3[:, 1, :], dtv, ALU.mult, r=["lam3", "dtv"], w=["ai"])
            self.tt("dve", argr, ar[:, :, None].to_broadcast([128, NG, NEXP]), exps[:, None, 0:NEXP].to_broadcast([128, NG, NEXP]),
                    ALU.mult, r=["ar", "exps"], w=["argr"])
            self.tt("dve", argi, ai[:, :, None].to_broadcast([128, NG, NEXP]), exps[:, None, 0:NEXP].to_broadcast([128, NG, NEXP]),
                    ALU.mult, r=["ai", "exps"], w=["argi"])
            self.act(mag, argr_f, AF.Exp, r=["argr"], w=["mag"])

            def sincos(dst, shift, key):
                self.ts("dve", kf, argi_f, shift, 1.0 / TWO_PI, ALU.add, ALU.mult, r=["argi", "mag"], w=["kf", "argr"])
                self.cp("dve", ki, kf, r=["kf"], w=["ki"])
                self.cp("dve", kf, ki, r=["ki"], w=["kf"])
                P.op("dve", lambda e: e.tensor_scalar_add(out=rr, in0=argi_f, scalar1=shift), r=["argi"], w=["rr"])
                self.stt_("dve", rr, kf, -CW1, rr, ALU.mult, ALU.add, r=["kf", "rr"], w=["rr"])
                self.stt_("dve", rr, kf, -CW2, rr, ALU.mult, ALU.add, r=["kf", "rr"], w=["rr"])
                self.ts("dve", rr, rr, 3.1415925, -3.1415925, ALU.min, ALU.max, r=["rr"], w=["rr"])
                self.act(dst, rr, AF.Sin, r=["rr"], w=[key])
            Tr_f = XBf[:, 0:NTB]
            Ti_f = XBf[:, NTB:2 * NTB]
            sincos(trg, 1.5707963267948966, "trg")
            self.tt("dve", Tr_f, mag, trg, ALU.mult, r=["mag", "trg"], w=["Tr"])
            sincos(trg, 0.0, "trg")
            self.tt("dve", Ti_f, mag, trg, ALU.mult, r=["mag", "trg"], w=["Ti"])
            L1r, L1i = t32a, t32b
            self.cp("dve", L1r[0:64, :], Tr[0:64, :, 8], r=["Tr"], w=["L1r"])
            self.cp("dve", L1r[64:128, :], Tr[64:128, :, 1], r=["Tr"], w=["L1r"])
            self.cp("dve", L1i[0:64, :], Ti[0:64, :, 8], r=["Ti"], w=["L1i"])
            self.cp("dve", L1i[64:128, :], Ti[64:128, :, 1], r=["Ti"], w=["L1i"])
            lr, li = lam3[:, 0, :], lam3[:, 1, :]
            P.op("dve", lambda e: e.tensor_scalar_add(out=L1r, in0=L1r, scalar1=-1.0), r=["L1r"], w=["L1r"])
            self.tt("dve", den, lr, lr, ALU.mult, r=["lam3"], w=["den"])
            self.tt("dve", qr, li, li, ALU.mult, r=["lam3"], w=["qr"])
            self.tt("dve", den, den, qr, ALU.add, r=["den", "qr"], w=["den"])
            P.op("dve", lambda e: e.reciprocal(out=den, in_=den), r=["den"], w=["den"])
            self.tt("dve", qr, L1r, lr, ALU.mult, r=["L1r", "lam3"], w=["qr"])
            self.tt("dve", qi, L1i, li, ALU.mult, r=["L1i", "lam3"], w=["qi"])
            self.tt("dve", qr, qr, qi, ALU.add, r=["qr", "qi"], w=["qr"])
            self.tt("dve", qi, L1i, lr, ALU.mult, r=["L1i", "lam3"], w=["qi"])
            self.tt("dve", L1i, L1r, li, ALU.mult, r=["L1r", "lam3"], w=["L1i"])
            self.tt("dve", qi, qi, L1i, ALU.subtract, r=["qi", "L1i"], w=["qi"])
            self.tt("dve", qr, qr, den, ALU.mult, r=["qr", "den"], w=["qr"])
            self.tt("dve", qi, qi, den, ALU.mult, r=["qi", "den"], w=["qi"])
            qrb = qr[:, :, None].to_broadcast([128, NG, 16])
            qib = qi[:, :, None].to_broadcast([128, NG, 16])
            self.tt("dve", btmp[:, 0], Bb[:, 0], qrb, ALU.mult, r=["Bb", "qr"], w=["btmp0"])
            self.tt("dve", btmp[:, 1], Bb[:, 1], qib, ALU.mult, r=["Bb", "qi"], w=["btmp1"])
            self.tt("dve", btmp[:, 0], btmp[:, 0], btmp[:, 1], ALU.subtract, r=["btmp0", "btmp1"], w=["btmp0"])
            self.tt("dve", btmp[:, 1], Bb[:, 1], qrb, ALU.mult, r=["Bb", "qr"], w=["btmp1"])
            self.tt("dve", Bb[:, 1], Bb[:, 0], qib, ALU.mult, r=["Bb", "qi"], w=["Bb"])
            self.tt("dve", Bb[:, 1], Bb[:, 1], btmp[:, 1], ALU.add, r=["Bb", "btmp1"], w=["Bb"])
            self.cp("dve", Bb[:, 0], btmp[:, 0], r=["btmp0", "Bb"], w=["Bb"])
            P.barrier()
            P.dma("sp", tabio, XBf[:, 0:6144], is_out=True)
        if getattr(self, "tdbg", None) is not None:
            P.dma("sp", self.tdbg, XBf[:, 0:3712], is_out=True)
        A8r2 = sm[:, 0:32].rearrange("p (c g) -> p c g", c=2)
        A8i = sm[:, 32:48]
        A8n = sm[:, 48:64]
        t1 = sm[:, 64:96].rearrange("p (c g) -> p c g", c=2)
        t2 = sm[:, 96:128].rearrange("p (c g) -> p c g", c=2)
        car = sm[:, 128:160].rearrange("p (c g) -> p c g", c=2)
        cm = sm[:, 160:192].rearrange("p (c g) -> p c g", c=2)
        A2r = sm[:, 192:208]
        A2i = sm[:, 208:224]
        hsel = sm[:, 224:226]
        P.op("dve", lambda e: e.memset(hsel, 0.0), w=["hsel"])
        P.op("dve", lambda e: e.memset(hsel[0:64, 0:1], 1.0), r=["hsel"], w=["hsel"])
        P.op("dve", lambda e: e.memset(hsel[64:128, 1:2], 1.0), r=["hsel"], w=["hsel"])
        TMf = self.tmpf[:, :, :].rearrange("p a d -> p (a d)")
        ZTf = self.zt[:, :, :].rearrange("p a d -> p (a d)")

        def gbuf(par):
            b = TMf if par == 0 else ZTf
            return dict(Wn=b[:, 0:256].rearrange("p (c m) -> p c m", c=2), Mo=b[:, 256:512].rearrange("p (c m) -> p c m", c=2),
                        Xn=b[:, 512:768].rearrange("p (c m) -> p c m", c=2), WT=b[:, 768:1024].rearrange("p (c m) -> p c m", c=2),
                        Mi=b[:, 1024:1152], tg=b[:, 1152:1408].rearrange("p (c m) -> p c m", c=2),
                        tg2=b[:, 1408:1536], XF=b[:, 1536:1792].rearrange("p (c m) -> p c m", c=2),
                        XBk=b[:, 1792:2048].rearrange("p (c m) -> p c m", c=2))

        def cmul(eng, dst, Pr, Pi, Qr, Qi, tmp, key, rk, neg_im=False):
            tk = key[:2] + "tg"
            self.tt(eng, dst[:, 0], Pr, Qr, ALU.mult, r=rk, w=[key + "0"])
            self.tt(eng, tmp[:, 0], Pi, Qi, ALU.mult, r=rk, w=[tk])
            self.tt(eng, dst[:, 0], dst[:, 0], tmp[:, 0], ALU.subtract, r=[key + "0", tk], w=[key + "0"])
            self.tt(eng, dst[:, 1], Pr, Qi, ALU.mult, r=rk, w=[key + "1"])
            self.tt(eng, tmp[:, 1], Pi, Qr, ALU.mult, r=rk, w=[tk])
            if neg_im:
                self.tt(eng, dst[:, 1], dst[:, 1], tmp[:, 1], ALU.add, r=[key + "1", tk], w=[key + "1"])
                d1 = dst[:, 1]
                self.P.op(eng, lambda e, d1=d1: e.tensor_scalar_mul(out=d1, in0=d1, scalar1=-1.0), r=[key + "1"], w=[key + "1"])
            else:
                self.tt(eng, dst[:, 1], dst[:, 1], tmp[:, 1], ALU.add, r=[key + "1", tk], w=[key + "1"])

        def v3(a):
            return a.rearrange("p (s h) -> p s h", s=8)

        def tab(T, g, e0):
            return T[:, g, e0:e0 + 8][:, :, None].to_broadcast([128, 8, 16])

        def par(B, c, g):
            return B[:, c, g, :][:, None, :].to_broadcast([128, 8, 16])

        def gen_W(g):
            gb = gbuf(g % 2)
            k = "g%d" % (g % 2)
            Wn = gb["Wn"]
            W3 = Wn.rearrange("p c (s h) -> p c s h", s=8)
            tg3 = gb["tg"].rearrange("p c (s h) -> p c s h", s=8)
            cmul("pool", W3, tab(Tr, g, 0), tab(Ti, g, 0), par(Bb, 0, g), par(Bb, 1, g), tg3, k + "W", ["Tr", "Ti", "Bb", "Bb"])
            pw = self.psf[4]

            def tr(e):
                e.transpose(pw[:, 0:128], Wn[:, 0, :], self.idf[:])
                return e.transpose(pw[:, 128:256], Wn[:, 1, :], self.idf[:])
            P.op("pe", tr, r=[k + "W0", k + "W1", "idf"], w=[("ps", 6)])
            self.cp("dve", gb["WT"].rearrange("p c m -> p (c m)"), pw[:, 0:256], r=[("ps", 6)], w=[k + "WT"])
            return gb

        def gen_M(g):
            gb = gbuf(g % 2)
            k = "g%d" % (g % 2)
            Mo3 = gb["Mo"].rearrange("p c (s h) -> p c s h", s=8)
            Xn3 = gb["Xn"].rearrange("p c (s h) -> p c s h", s=8)
            tg3 = gb["tg"].rearrange("p c (s h) -> p c s h", s=8)
            cmul("pool", Mo3, tab(Tr, g, 8), tab(Ti, g, 8), par(Cc, 0, g), par(Cc, 1, g), tg3, k + "Mo", ["Tr", "Ti", "Cc"], neg_im=True)
            cmul("pool", Xn3, tab(Tr, g, 16), tab(Ti, g, 16), par(Bb, 0, g), par(Bb, 1, g), tg3, k + "Xn", ["Tr", "Ti", "Bb", "Bb"])
            pw = self.psf[4]
            Mo, Xn = gb["Mo"], gb["Xn"]
            XF, XBk = gb["XF"], gb["XBk"]
            P.op("dve", lambda e: e.tensor_scalar_mul(out=XF, in0=Xn, scalar1=hsel[:, 0:1]), r=[k + "Xn0", k + "Xn1", "hsel"], w=[k + "XF"])
            P.op("dve", lambda e: e.tensor_scalar_mul(out=XBk, in0=Xn, scalar1=hsel[:, 1:2]), r=[k + "Xn0", k + "Xn1", "hsel"], w=[k + "XB"])

            def mm(e):
                ins = None
                for hf, Xm in enumerate((XF, XBk)):
                    e.matmul(pw[:, hf * 128:(hf + 1) * 128], lhsT=Xm[:, 0, :], rhs=Mo[:, 0, :], start=True, stop=False)
                    ins = e.matmul(pw[:, hf * 128:(hf + 1) * 128], lhsT=Xm[:, 1, :], rhs=Mo[:, 1, :], start=False, stop=True)
                return ins
            P.op("pe", mm, r=[k + "Mo0", k + "Mo1", k + "XF", k + "XB"], w=[("ps", 6)])
            Mi = gb["Mi"]
            self.tt("dve", gb["tg2"], pw[:, 0:128], mLU[:, 0, :], ALU.mult, r=[("ps", 6), "mLU"], w=[k + "tg2"])
            self.tt("dve", Mi, pw[:, 128:256], mLU[:, 1, :], ALU.mult, r=[("ps", 6), "mLU"], w=[k + "Mi"])
            self.tt("dve", Mi, Mi, gb["tg2"], ALU.add, r=[k + "Mi", k + "tg2"], w=[k + "Mi"])
            self.stt_("dve", Mi, self.idf[:], Dg[:, g:g + 1], Mi, ALU.mult, ALU.add, r=[k + "Mi", "idf", "Dg"], w=[k + "Mi"])
            return gb

        for hs in range(2):
            G0 = hs * 16
            self.cp("dve", A8r2[:, 0, :], Tr[:, G0:G0 + 16, 24], r=["Tr"], w=["A8r2"])
            self.cp("dve", A8r2[:, 1, :], Tr[:, G0:G0 + 16, 24], r=["Tr"], w=["A8r2"])
            self.cp("dve", A8i, Ti[:, G0:G0 + 16, 24], r=["Ti"], w=["A8i"])
            P.op("dve", lambda e, G0=G0: e.tensor_scalar_mul(out=A8n, in0=Ti[:, G0:G0 + 16, 24], scalar1=-1.0), r=["Ti"], w=["A8n"])
            self.cp("dve", A2r, Tr[:, G0:G0 + 16, 25], r=["Tr"], w=["A2r"])
            self.cp("dve", A2i, Ti[:, G0:G0 + 16, 25], r=["Ti"], w=["A2i"])
            if mode == "B":
                P.dma("sp", STf[:, 0:9344], stio[hs], w=[("ST", 0), ("ST", 1)])
            else:
                P.op("pool", lambda e: e.memset(STf[:, 0:9344], 0.0), w=[("ST", 0), ("ST", 1)])
            self.chk(1)
            for gl in (range(16) if mode == "A" else []):
                g = G0 + gl
                gb = gen_W(g)
                pr, pi = self.psf[(gl % 2) * 2], self.psf[(gl % 2) * 2 + 1]
                br = 2 + (gl % 2) * 2
                k = "g%d" % (g % 2)
                P.op("pe", lambda e, pr=pr, gb=gb, g=g: e.matmul(pr[:, 0:NCH], lhsT=gb["WT"][:, 0, :], rhs=UT8[:, g, :], start=True, stop=True),
                     r=[k + "WT"], w=[("ps", br)])
                P.op("pe", lambda e, pi=pi, gb=gb, g=g: e.matmul(pi[:, 0:NCH], lhsT=gb["WT"][:, 1, :], rhs=UT8[:, g, :], start=True, stop=True),
                     r=[k + "WT"], w=[("ps", br + 1)])
                for c, pp in enumerate((pr, pi)):
                    e1 = "act" if c == 0 else "dve"
                    self.cp(e1, ST[0:64, c, gl, 2:258], pp[0:64, 0:256], r=[("ps", br + c)], w=[("ST", 0)])
                    self.cp(e1, ST[0:64, c, gl, 260:292], pp[0:64, 256:288], r=[("ps", br + c)], w=[("ST", 0)])
                    self.cp(e1, ST[64:128, c, gl, 0:256], pp[64:128, 0:256], r=[("ps", br + c)], w=[("ST", 1)])
                    self.cp(e1, ST[64:128, c, gl, 258:290], pp[64:128, 256:288], r=[("ps", br + c)], w=[("ST", 1)])
            self.chk(2)
            def step(half, src, dst):
                eng = "dve" if half == 0 else "pool"
                sl = slice(half * 64, half * 64 + 64)
                h = "h%d" % half
                cur = ST[sl, :, :, src]
                nxt = ST[sl, :, :, dst]
                self.tt(eng, t1[sl], cur, A8r2[sl], ALU.mult, r=[("ST", half), "A8r2"], w=["t1" + h])
                self.tt(eng, t2[sl, 0, :], cur[:, 1, :], A8n[sl], ALU.mult, r=[("ST", half), "A8n"], w=["t2a" + h])
                self.tt(eng, t2[sl, 1, :], cur[:, 0, :], A8i[sl], ALU.mult, r=[("ST", half), "A8i"], w=["t2b" + h])
                self.tt(eng, t1[sl], t1[sl], t2[sl], ALU.add, r=["t1" + h, "t2a" + h, "t2b" + h], w=["t1" + h])
                self.tt(eng, nxt, nxt, t1[sl], ALU.add, r=["t1" + h, ("ST", half)], w=[("ST", half)])

            if mode == "A":
                for c in range(32):
                    step(0, 259 + c, 260 + c)
                    step(1, 290 - c, 289 - c)
            if mode == "B":
                self.chk(3)
                self.cp("dve", car[0:64], ST[0:64, :, :, 291], r=[("ST", 0)], w=["car"])
                self.cp("dve", car[64:128], ST[64:128, :, :, 258], r=[("ST", 1)], w=["car"])
                for s in range(3):
                    self.tt("dve", cm[:, 0, :], car[:, 0, :], A2r, ALU.mult, r=["car", "A2r"], w=["cm"])
                    self.tt("dve", t1[:, 0, :], car[:, 1, :], A2i, ALU.mult, r=["car", "A2i"], w=["t1h0", "t1h1"])
                    self.tt("dve", cm[:, 0, :], cm[:, 0, :], t1[:, 0, :], ALU.subtract, r=["cm", "t1h0", "t1h1"], w=["cm"])
                    self.tt("dve", cm[:, 1, :], car[:, 1, :], A2r, ALU.mult, r=["car", "A2r"], w=["cm"])
                    self.tt("dve", t1[:, 1, :], car[:, 0, :], A2i, ALU.mult, r=["car", "A2i"], w=["t1h0", "t1h1"])
                    self.tt("dve", cm[:, 1, :], cm[:, 1, :], t1[:, 1, :], ALU.add, r=["cm", "t1h0", "t1h1"], w=["cm"])
                    self.tt("dve", cm, cm, car, ALU.subtract, r=["cm", "car"], w=["cm"])
                    self.stt_("dve", car, cm, af[:, s:s + 1], car, ALU.mult, ALU.add, r=["cm", "car", "af"], w=["car"])
                    self.tt("dve", car, car, El[:, s, :, G0:G0 + 16], ALU.add, r=["car", "El"], w=["car"])
                self.cp("dve", ST[0:64, :, :, 1], car[0:64], r=["car"], w=[("ST", 0)])
                self.cp("dve", ST[64:128, :, :, 256], car[64:128], r=["car"], w=[("ST", 1)])
            self.chk(4)
            T1 = HBf[:, 0:512].rearrange("p (c g b) -> p c g b", c=2, g=16)
            T2 = HBf[:, 512:1024].rearrange("p (c g b) -> p c g b", c=2, g=16)
            CIN = XBf[:, y8off:y8off + 544].rearrange("p (c g b) -> p c g b", c=2, g=16)

            def cma(half, cur, nxt, ecol, t1v, t2v, kcur, knxt, ktmp):
                eng = "dve" if half == 0 else "pool"
                sl = slice(half * 64, half * 64 + 64)
                shp = list(cur.shape)
                crs = Tr[sl, G0:G0 + 16, ecol]
                cis = Ti[sl, G0:G0 + 16, ecol]
                if len(shp) == 4:
                    crb = crs[:, None, :, None].to_broadcast(shp)
                    cib = cis[:, None, :, None].to_broadcast(shp)
                else:
                    crb = crs[:, None, :].to_broadcast(shp)
                    cib = cis[:, None, :].to_broadcast(shp)
                self.tt(eng, t1v, cur, crb, ALU.mult, r=kcur + ["Tr"], w=[ktmp + "1"])
                self.tt(eng, t2v, cur, cib, ALU.mult, r=kcur + ["Ti"], w=[ktmp + "2"])
                self.tt(eng, t1v[:, 0], t1v[:, 0], t2v[:, 1], ALU.subtract, r=[ktmp + "1", ktmp + "2"], w=[ktmp + "1"])
                self.tt(eng, t1v[:, 1], t1v[:, 1], t2v[:, 0], ALU.add, r=[ktmp + "1", ktmp + "2"], w=[ktmp + "1"])
                self.tt(eng, nxt, nxt, t1v, ALU.add, r=[ktmp + "1"] + knxt, w=knxt)

            P.barrier()
            kS = [[("ST", 0)], [("ST", 1)]]
            kC = [[("CIN", 0)], [("CIN", 1)]]
            hsl = [slice(0, 64), slice(64, 128)]
            for j in (range(1, 16) if mode == "A" else []):
                cma(0, ST[hsl[0], :, :, j + 1:j + 242:16], ST[hsl[0], :, :, j + 2:j + 243:16], 24,
                    T1[hsl[0]], T2[hsl[0]], kS[0], kS[0], "HB0")
                jb = 15 - j
                cma(1, ST[hsl[1], :, :, jb + 1:jb + 242:16], ST[hsl[1], :, :, jb:jb + 241:16], 24,
                    T1[hsl[1]], T2[hsl[1]], kS[1], kS[1], "HB1")
            if mode == "A":
                P.dma("sp", stio[hs], STf[:, 0:9344], r=kS[0] + kS[1], is_out=True)
            self.cp("dve", CIN[hsl[0], :, :, 0], ST[hsl[0], :, :, 1], r=kS[0] + [("Y8", 0), ("Y8", 1)], w=kC[0])
            self.cp("pool", CIN[hsl[1], :, :, 15], ST[hsl[1], :, :, 256], r=kS[1] + [("Y8", 0), ("Y8", 1)], w=kC[1])
            for b in range(16):
                self.cp("dve", CIN[hsl[0], :, :, b + 1], ST[hsl[0], :, :, 16 * b + 17], r=kS[0], w=kC[0])
                cma(0, CIN[hsl[0], :, :, b], CIN[hsl[0], :, :, b + 1], 41, t1[hsl[0]], t2[hsl[0]], kC[0], kC[0], "t1h0")
                if b < 15:
                    bb_ = 15 - b
                    self.cp("pool", CIN[hsl[1], :, :, bb_ - 1], ST[hsl[1], :, :, 16 * bb_], r=kS[1], w=kC[1])
                    cma(1, CIN[hsl[1], :, :, bb_], CIN[hsl[1], :, :, bb_ - 1], 41, t1[hsl[1]], t2[hsl[1]], kC[1], kC[1], "t1h1")
            if mode == "A":
                self.cp("pool", CIN[hsl[1], :, :, 16], ST[hsl[1], :, :, 0], r=kS[1], w=kC[1])
                cma(1, CIN[hsl[1], :, :, 0], CIN[hsl[1], :, :, 16], 41, t1[hsl[1]], t2[hsl[1]], kC[1], kC[1], "t1h1")
                self.cp("dve", car[0:64], CIN[hsl[0], :, :, 16], r=kC[0], w=["car"])
                self.cp("dve", car[64:128], CIN[hsl[1], :, :, 16], r=kC[1], w=["car"])
                P.dma("sp", eout[hs], car, r=["car"], is_out=True)
                continue
            for j in range(16):
                cma(0, CIN[hsl[0], :, :, 0:16], ST[hsl[0], :, :, j + 2:j + 243:16], 26 + j,
                    T1[hsl[0]], T2[hsl[0]], kC[0], kS[0], "HB0")
                cma(1, CIN[hsl[1], :, :, 0:16], ST[hsl[1], :, :, j:j + 241:16], 26 + (15 - j),
                    T1[hsl[1]], T2[hsl[1]], kC[1], kS[1], "HB1")
            self.chk(5)
            for gl in range(16):
                g = G0 + gl
                gb = gen_M(g)
                if gl == 0:
                    self.chk(6)
                k = "g%d" % (g % 2)
                py = self.psf[(gl % 2) * 2]
                by = 2 + (gl % 2) * 2

                def mm(e, gb=gb, g=g, gl=gl, py=py):
                    ins = None
                    for (c0, c1, s0) in ((0, 256, 1), (256, 288, 259)):
                        n = c1 - c0
                        e.matmul(py[:, c0:c1], lhsT=gb["Mi"], rhs=UT8[:, g, c0:c1], start=True, stop=False)
                        e.matmul(py[:, c0:c1], lhsT=gb["Mo"][:, 0, :], rhs=ST[:, 0, gl, s0:s0 + n], start=False, stop=False)
                        ins = e.matmul(py[:, c0:c1], lhsT=gb["Mo"][:, 1, :], rhs=ST[:, 1, gl, s0:s0 + n], start=False, stop=True)
                    return ins
                P.op("pe", mm, r=[k + "Mi", k + "Mo0", k + "Mo1", ("ST", 0), ("ST", 1)], w=[("ps", by)])
                y8 = Y8[gl % 2]
                self.cp("act", y8, py[:, 0:NCH], r=[("ps", by)], w=[("Y8", gl % 2), ("CIN", 0), ("CIN", 1)])
                pt = self.psf[5]

                def tr(e, y8=y8):
                    e.transpose(pt[:, 0:128], y8[:, 0:128], self.idf[:])
                    e.transpose(pt[:, 128:256], y8[:, 128:256], self.idf[:])
                    return e.transpose(pt[0:32, 256:384], y8[:, 256:288], self.idf[:])
                P.op("pe", tr, r=[("Y8", gl % 2), "idf"], w=[("ps", 7)])
                if gl == 0:
                    self.chk(7)
                gq = g % 8
                for b in range(3):
                    rows = 128 if b < 2 else 32
                    self.cp("act", Gt32[0:rows, b, :, gq * 16:(gq + 1) * 16],
                            pt[0:rows, b * 128:(b + 1) * 128].rearrange("p (t h) -> p t h", t=8), r=[("ps", 7)], w=[("Gt", b)])
                if gq == 7:
                    kc = g // 8
                    for b in range(3):
                        rows = 128 if b < 2 else 32
                        xg = Gt32[0:rows, b].rearrange("p t c -> p (t c)")
                        ug = HBf[0:rows, :]
                        self.tt("dve", ug, xg, xg, ALU.mult, r=[("Gt", b), "HB01", "HB02", "HB11", "HB12"], w=["ug", "HB01", "HB02", "HB11", "HB12"])
                        self.ts("dve", ug, ug, 0.044715, 1.0, ALU.mult, ALU.add, r=["ug"], w=["ug"])
                        self.tt("dve", ug, ug, xg, ALU.mult, r=["ug", ("Gt", b)], w=["ug"])
                        self.act(ug, ug, AF.Sigmoid, r=["ug"], w=["ug"], scale=1.5957691216057308)
                        self.tt("dve", Gb[0:rows, b].rearrange("p t c -> p (t c)"), xg, ug, ALU.mult, r=["ug", ("Gt", b)], w=[("Gb", b)])
                        pb = self.psb[b % 2]

                        def tr2(e, b=b, rows=rows, pb=pb):
                            ins = None
                            for t in range(8):
                                ins = e.transpose(pb[:, t * 128:t * 128 + rows], Gb[0:rows, b, t, :], self.idb[0:rows, 0:rows])
                            return ins
                        P.op("pe", tr2, r=[("Gb", b), "idb"], w=[("ps", b % 2)])
                        dst = gT[:, kc, b * 1024:b * 1024 + rows * 8].rearrange("p (c t) -> p t c", t=8)
                        self.cp("dve", dst, pb[:, :].rearrange("p (t c) -> p t c", t=8)[:, :, 0:rows], r=[("ps", b % 2)], w=[("gT", kc, b)])
        if mode == "A":
            return
        self.chk(8)
        P.barrier()
        wgl = R[:, 0:2048].rearrange("p (k c) -> p k c", k=4)
        sgb = [R[:, 2048 + i * 1024:2048 + (i + 1) * 1024].bitcast(F32) for i in range(2)]
        P.dma("pool", wgl, wglu.rearrange("(k p) c -> p k c", p=128), w=["wgl"])
        ci = 0
        for oc in range(4):
            for (n0, n) in ((0, 512), (512, 512), (1024, 512), (1536, 512), (2048, 256)):
                pp = self.psf[ci % 2]

                def mm(e, oc=oc, n0=n0, n=n, pp=pp):
                    ins = None
                    for kc in range(4):
                        ins = e.matmul(pp[:, 0:n], lhsT=wgl[:, kc, oc * 128:(oc + 1) * 128], rhs=gT[:, kc, n0:n0 + n],
                                       start=(kc == 0), stop=(kc == 3))
                    return ins
                P.op("pe", mm, r=["wgl"], w=[("ps", 2 + ci % 2)])
                self.act(sgb[ci % 2][:, 0:n], pp[:, 0:n], AF.Sigmoid, r=[("ps", 2 + ci % 2)], w=[("sgb", ci % 2)])
                self.tt("dve", yaT[:, oc, n0:n0 + n], gT[:, oc, n0:n0 + n], sgb[ci % 2][:, 0:n], ALU.mult,
                        r=[("sgb", ci % 2)], w=[("yaT", oc, n0)])
                ci += 1
        P.barrier()

    HT_FULL = 22

    def build_hT_full(self, x1, xh, xkey="x1"):
        R = self.R
        hT = R[:, 0:22528].rearrange("p (k t) -> p k t", k=8)
        ents = []
        for i in range(2):
            ents.append((xh[i * 128:(i + 1) * 128, :], 0, i, None))
        for t in range(NT_LAT):
            ents.append((x1[t * 128:(t + 1) * 128, :], 0, 2 + t, (xkey, t)))
        for i in range(2):
            ents.append((xh[256 + i * 128:256 + (i + 1) * 128, :], 0, 18 + i, None))
        for i in range(2):
            t = NT_LAT + i
            ents.append((x1[t * 128:(t + 1) * 128, :], 1, 20 + i, (xkey, t)))
        self.make_hT(hT, ents)
        return hT

    def attn_phase(self, hT, win, biasT, maskT):
        P = self.P
        R = self.R
        o = 22528
        kT = R[:, o:o + 2816]; o += 2816
        qT = R[:, o:o + 2304]; o += 2304
        Va = R[:, o:o + 2860].rearrange("p (i h d) -> p i h d", i=22, h=2); o += 2860
        wk = R[:, o:o + 3072].rearrange("p (k c) -> p k c", k=8); o += 3072
        bb = R[:, o:o + 3584].rearrange("p (h j x c) -> p h j x c", h=2, j=7, x=2); o += 3584
        ybt = [R[:, o + i * 128:o + (i + 1) * 128].rearrange("p (h d) -> p h d", h=2) for i in range(2)]; o += 256
        rec = self.Rf(o, 4); o += 8
        assert o <= 38352
        ybT = R[:, 38352:47568].rearrange("p (k t) -> p k t", k=4)
        o = 47568
        bst = self.Rf(o, 1792).rearrange("p (h j c) -> p h j c", h=2, j=7); o += 3584
        mb = R[:, o:o + 3456].rearrange("p (i c) -> p i c", i=27); o += 3456
        ET = [R[:, o + i * 1024:o + (i + 1) * 1024] for i in range(2)]; o += 2048
        assert o <= 56784
        P.dma("pool", mb, maskT, w=["mb"])
        P.op("pool", lambda e: e.memset(Va[:, :, :, 64:65], 1.0), w=["Va1"])
        mcls = {0: (0, list(range(0, 6))), 1: (6, list(range(0, 5))), 14: (16, list(range(0, 5))), 15: (21, list(range(-1, 5)))}
        ev = 0
        for hp in range(4):
            for i, col in enumerate((COL_K, COL_V, COL_Q)):
                P.dma("pool", wk[:, :, i * 128:(i + 1) * 128],
                      win[:, col + hp * 128:col + (hp + 1) * 128].rearrange("(k p) c -> p k c", p=128), w=[("wk", i)])
            P.dma("sp", bst, biasT[hp], w=["bst"])
            self.cp("dve", bb[:, :, :, 0, :], bst, r=["bst"], w=["bb0"])
            self.tt("dve", bst, bst, bb[:, :, :, 0, :], ALU.subtract, r=["bst", "bb0"], w=["bst"])
            self.cp("dve", bb[:, :, :, 1, :], bst, r=["bst"], w=["bb1"])
            for (n0, n) in ((0, 512), (512, 512), (1024, 512), (1536, 512), (2048, 512), (2560, 256)):
                pp = self.psf[ev % 2]

                def mm(e, n0=n0, n=n, pp=pp):
                    ins = None
                    for k in range(8):
                        ins = e.matmul(pp[:, 0:n], lhsT=wk[:, k, 0:128], rhs=hT[:, k, n0:n0 + n], start=(k == 0), stop=(k == 7))
                    return ins
                P.op("pe", mm, r=[("wk", 0)], w=[("ps", 2 + ev % 2)])
                self.cp("act" if ev % 2 == 0 else "dve", kT[:, n0:n0 + n], pp[:, 0:n], r=[("ps", 2 + ev % 2)], w=["kT"])
                ev += 1
            for (h0, q0, n) in ((256, 0, 512), (768, 512, 512), (1280, 1024, 512), (1792, 1536, 512), (2560, 2048, 256)):
                pp = self.psf[ev % 2]

                def mm(e, h0=h0, n=n, pp=pp):
                    ins = None
                    for k in range(8):
                        ins = e.matmul(pp[:, 0:n], lhsT=wk[:, k, 256:384], rhs=hT[:, k, h0:h0 + n], start=(k == 0), stop=(k == 7))
                    return ins
                P.op("pe", mm, r=[("wk", 2)], w=[("ps", 2 + ev % 2)])
                P.op("dve", lambda e, q0=q0, n=n, pp=pp: e.tensor_scalar_mul(out=qT[:, q0:q0 + n], in0=pp[:, 0:n], scalar1=0.125),
                     r=[("ps", 2 + ev % 2)], w=["qT"])
                ev += 1
            for i0 in range(0, 22, 4):
                ni = min(4, 22 - i0)
                pp = self.psf[ev % 2]

                def mm(e, i0=i0, ni=ni, pp=pp):
                    ins = None
                    for ii in range(ni):
                        i = i0 + ii
                        for k in range(8):
                            ins = e.matmul(pp[:, ii * 128:(ii + 1) * 128], lhsT=hT[:, k, i * 128:(i + 1) * 128], rhs=wk[:, k, 128:256],
                                           start=(k == 0), stop=(k == 7))
                    return ins
                P.op("pe", mm, r=[("wk", 1)], w=[("ps", 2 + ev % 2)])
                self.cp("act" if ev % 2 == 0 else "dve", Va[:, i0:i0 + ni, :, 0:64],
                        pp[:, 0:ni * 128].rearrange("p (i h d) -> p i h d", i=ni, h=2), r=[("ps", 2 + ev % 2)], w=["Va"])
                ev += 1
            for m in range(18):
                po = self.psf[4 + m % 2]
                for hh in range(2):
                    sl = slice(hh * 64, hh * 64 + 64)
                    it = (m * 2 + hh) % 2
                    pa, pbk = self.psf[it * 2], self.psf[it * 2 + 1]
                    if m < 16:
                        moff, jos = mcls.get(m, (11, list(range(0, 5))))
                        chunks = [(m + jo, jo, moff + ji) for ji, jo in enumerate(jos)] + [(20, None, None), (21, None, None)]
                    else:
                        chunks = [(20, None, None), (21, None, None)]
                    nchk = len(chunks)

                    def mm(e, chunks=chunks, sl=sl, hh=hh, m=m, pa=pa, pbk=pbk):
                        ins = None
                        for c, (ti, jo, mi) in enumerate(chunks):
                            dst = (pa if c < 4 else pbk)[:, (c % 4) * 128:(c % 4 + 1) * 128]
                            ins = e.matmul(dst, lhsT=kT[sl, ti * 128:(ti + 1) * 128], rhs=qT[sl, m * 128:(m + 1) * 128],
                                           start=True, stop=(jo is None))
                            if jo is not None:
                                e.matmul(dst, lhsT=self.idb[:], rhs=bb[:, hh, jo + 1, 0, :], start=False, stop=False)
                                e.matmul(dst, lhsT=self.idb[:], rhs=bb[:, hh, jo + 1, 1, :], start=False, stop=False)
                                ins = e.matmul(dst, lhsT=self.idb[:], rhs=mb[:, mi, :], start=False, stop=True)
                        return ins
                    P.op("pe", mm, r=["kT", "qT", "bb0", "bb1", "mb", "idb"], w=[("ps", 2 + it * 2), ("ps", 3 + it * 2)])
                    n1 = min(nchk, 4) * 128
                    n2 = (nchk - 4) * 128
                    self.act(ET[it][:, 0:n1], pa[:, 0:n1], AF.Exp, r=[("ps", 2 + it * 2)], w=[("ET", it)])
                    if n2 > 0:
                        self.act(ET[it][:, 512:512 + n2], pbk[:, 0:n2], AF.Exp, r=[("ps", 3 + it * 2)], w=[("ET", it)])

                    def pv(e, chunks=chunks, hh=hh, it=it, po=po):
                        ins = None
                        for c, (ti, jo, mi) in enumerate(chunks):
                            ins = e.matmul(po[:, hh * 65:(hh + 1) * 65], lhsT=ET[it][:, c * 128:(c + 1) * 128], rhs=Va[:, ti, hh, :],
                                           start=(c == 0), stop=(c == len(chunks) - 1))
                        return ins
                    P.op("pe", pv, r=[("ET", it), "Va", "Va1"], w=[("ps", 6 + m % 2)])
                pov = po[:, 0:130].rearrange("p (h d) -> p h d", h=2)
                P.op("dve", lambda e, pov=pov: e.reciprocal(out=rec[:, 0:2], in_=pov[:, :, 64]), r=[("ps", 6 + m % 2)], w=["rec"])
                yb = ybt[m % 2]
                self.tt("dve", yb, pov[:, :, 0:64], rec[:, 0:2][:, :, None].to_broadcast([128, 2, 64]), ALU.mult,
                        r=[("ps", 6 + m % 2), "rec"], w=[("ybt", m % 2)])
                pt = self.psb[m % 2]
                P.op("pe", lambda e, yb=yb, pt=pt: e.transpose(pt[:, 0:128], yb.rearrange("p h d -> p (h d)"), self.idb[:]),
                     r=[("ybt", m % 2), "idb"], w=[("ps", m % 2)])
                self.cp("act", ybT[:, hp, m * 128:(m + 1) * 128], pt[:, 0:128], r=[("ps", m % 2)], w=[("ybT", hp, m)])

    def conv_phase(self, hT, win, cw, cflag):
        P = self.P
        R = self.R
        ycT = R[:, 47568:56784].rearrange("p (k t) -> p k t", k=4)
        o = 22528
        wz = R[:, o:o + 3072].rearrange("p (k c) -> p k c", k=8); o += 3072
        vv = self.Rf(o, 2052); o += 4104
        vc = self.Rf(o, 260); o += 520
        zs = self.Rf(o, 512); o += 1024
        yt = self.Rf(o, 512); o += 1024
        cwt = self.Rf(o, 12).rearrange("p (c i) -> p c i", c=4); o += 24
        cfl = self.Rf(o, 2); o += 8
        assert o <= 38352
        P.barrier()
        P.dma("sp", cwt, cw, w=["cwt"])
        P.dma("sp", cfl, cflag, w=["cfl"])
        P.op("dve", lambda e: e.memset(vc[:, :], 0.0), w=["vc"])
        for cc in range(4):
            for i, col in enumerate((COL_Z, COL_C, COL_B)):
                P.dma("pool", wz[:, :, i * 128:(i + 1) * 128],
                      win[:, col + cc * 128:col + (cc + 1) * 128].rearrange("(k p) c -> p k c", p=128), w=[("wz", i)])

            def zc(h0, n, dst, dkey):
                for i in range(2):
                    pp = self.psf[i]

                    def mm(e, i=i, pp=pp):
                        ins = None
                        for k in range(8):
                            ins = e.matmul(pp[:, 0:n], lhsT=wz[:, k, i * 128:(i + 1) * 128], rhs=hT[:, k, h0:h0 + n],
                                           start=(k == 0), stop=(k == 7))
                        return ins
                    P.op("pe", mm, r=[("wz", i)], w=[("ps", 2 + i)])
                self.cp("act", zs[:, 0:n], self.psf[0][:, 0:n], r=[("ps", 2)], w=["zs"])
                self.tt("dve", dst, zs[:, 0:n], self.psf[1][:, 0:n], ALU.mult, r=["zs", ("ps", 3)], w=[dkey])
            for i in range(5):
                zc(255 + 410 * i, 410, vv[:, 410 * i:410 * (i + 1)], "vv")
            self.tt("dve", vv[:, 0:1], vv[:, 0:1], cfl[:, 0:1], ALU.mult, r=["vv", "cfl"], w=["vv"])
            self.tt("dve", vv[:, 2049:2050], vv[:, 2049:2050], cfl[:, 1:2], ALU.mult, r=["vv", "cfl"], w=["vv"])
            zc(2560, 256, vc[:, 1:257], "vc")

            def outp(src, s0, h0, n, dst, skey):
                pp = self.psf[2]

                def mm(e):
                    ins = None
                    for k in range(8):
                        ins = e.matmul(pp[:, 0:n], lhsT=wz[:, k, 256:384], rhs=hT[:, k, h0:h0 + n], start=(k == 0), stop=(k == 7))
                    return ins
                P.op("pe", mm, r=[("wz", 2)], w=[("ps", 4)])
                P.op("dve", lambda e, cc=cc: e.tensor_scalar_mul(out=yt[:, 0:n], in0=src[:, s0 + 1:s0 + 1 + n], scalar1=cwt[:, cc, 1:2]),
                     r=[skey, "cwt"], w=["yt"])
                self.stt_("dve", yt[:, 0:n], src[:, s0:s0 + n], cwt[:, cc, 0:1], yt[:, 0:n], ALU.mult, ALU.add, r=[skey, "cwt", "yt"], w=["yt"])
                self.stt_("dve", yt[:, 0:n], src[:, s0 + 2:s0 + 2 + n], cwt[:, cc, 2:3], yt[:, 0:n], ALU.mult, ALU.add, r=[skey, "cwt", "yt"], w=["yt"])
                self.tt("dve", dst, yt[:, 0:n], pp[:, 0:n], ALU.mult, r=["yt", ("ps", 4)], w=["ycT"])
            for i in range(4):
                outp(vv, 512 * i, 256 + 512 * i, 512, ycT[:, cc, 512 * i:512 * (i + 1)], "vv")
            outp(vc, 0, 2560, 256, ycT[:, cc, 2048:2304], "vc")

    def merge_phase(self, x1, x2, win, wbranch, wout, xkey="x1"):
        P = self.P
        R = self.R
        yT = [R[:, 56784:66000].rearrange("p (k t) -> p k t", k=4), R[:, 38352:47568].rearrange("p (k t) -> p k t", k=4),
              R[:, 47568:56784].rearrange("p (k t) -> p k t", k=4)]
        o = 0
        wbr = R[:, o:o + 12288].rearrange("p (b c) -> p b c", b=12); o += 12288
        hTp = R[:, o:o + 4096].rearrange("p (k t) -> p k t", k=8); o += 4096
        wgt = [R[:, o + i * 3072:o + (i + 1) * 3072].rearrange("p (k c) -> p k c", k=8) for i in range(2)]; o += 6144
        acc = self.Rf(o, 512); o += 1024
        sgm = [self.Rf(o + i * 1024, 512) for i in range(2)]; o += 2048
        mT = R[:, o:o + 4096].rearrange("p (f t) -> p f t", f=8); o += 4096
        wo = R[:, o:o + 8192].rearrange("p (f c) -> p f c", f=8); o += 8192
        assert o <= 38352
        P.barrier()
        for b in range(3):
            P.dma("pool", wbr[:, b * 4:(b + 1) * 4, :], wbranch[b].rearrange("(k p) c -> p k c", p=128), w=[("wbr", b)])
        P.dma("pool", wo, wout.rearrange("(f p) c -> p f c", p=128), w=["wo"])
        wc = 0
        ac = 0
        for (t0, ntl) in ((0, 4), (4, 4), (8, 4), (12, 4), (16, 2)):
            n = ntl * 128
            tk0 = t0 * 128
            slots = []
            for i in range(ntl):
                t = t0 + i
                v = 0 if t < NT_LAT else 1
                s = self.xload(x1[t * 128:(t + 1) * 128, :], dkey=(xkey, t))
                slots.append(s)
                self.modulate_T(s, v, hTp, i, i % 2)
            for fi in range(8):
                wb_ = wc % 2
                wc += 1
                for b in range(3):
                    P.dma("pool", wgt[wb_][:, :, b * 128:(b + 1) * 128],
                          win[:, COL_G + b * 1024 + fi * 128:COL_G + b * 1024 + (fi + 1) * 128].rearrange("(k p) c -> p k c", p=128),
                          w=[("wgt", wb_, b)])
                for b in range(3):
                    pA, pG = self.psf[(ac % 2) * 2], self.psf[(ac % 2) * 2 + 1]
                    bA = 2 + (ac % 2) * 2

                    def mmA(e, b=b, fi=fi, pA=pA, n=n, tk0=tk0):
                        ins = None
                        for kc in range(4):
                            ins = e.matmul(pA[:, 0:n], lhsT=wbr[:, b * 4 + kc, fi * 128:(fi + 1) * 128], rhs=yT[b][:, kc, tk0:tk0 + n],
                                           start=(kc == 0), stop=(kc == 3))
                        return ins
                    P.op("pe", mmA, r=[("wbr", b)], w=[("ps", bA)])

                    def mmG(e, b=b, wb_=wb_, pG=pG, n=n):
                        ins = None
                        for k in range(8):
                            ins = e.matmul(pG[:, 0:n], lhsT=wgt[wb_][:, k, b * 128:(b + 1) * 128], rhs=hTp[:, k, 0:n],
                                           start=(k == 0), stop=(k == 7))
                        return ins
                    P.op("pe", mmG, r=[("wgt", wb_, b)] + [("hT", i) for i in range(ntl)], w=[("ps", bA + 1)])
                    sg = sgm[ac % 2]
                    self.act(sg[:, 0:n], pG[:, 0:n], AF.Sigmoid, r=[("ps", bA + 1)], w=[("sgm", ac % 2)])
                    if b == 0:
                        self.tt("dve", acc[:, 0:n], sg[:, 0:n], pA[:, 0:n], ALU.mult, r=[("sgm", ac % 2), ("ps", bA)], w=["acc"])
                    else:
                        self.tt("dve", sg[:, 0:n], sg[:, 0:n], pA[:, 0:n], ALU.mult, r=[("sgm", ac % 2), ("ps", bA)], w=[("sgm", ac % 2)])
                        if b == 1:
                            self.tt("dve", acc[:, 0:n], acc[:, 0:n], sg[:, 0:n], ALU.add, r=[("sgm", ac % 2), "acc"], w=["acc"])
                        else:
                            self.tt("dve", mT[:, fi, 0:n], acc[:, 0:n], sg[:, 0:n], ALU.add, r=[("sgm", ac % 2), "acc"], w=[("mT", fi)])
                    ac += 1
            for i in range(ntl):
                t = t0 + i
                v = 0 if t < NT_LAT else 1

                def mmo(e, i=i):
                    ins = None
                    for dh in range(2):
                        for fi in range(8):
                            ins = e.matmul(self.psf[4 + dh][:, :], lhsT=mT[:, fi, i * 128:(i + 1) * 128], rhs=wo[:, fi, dh * 512:(dh + 1) * 512],
                                           start=(fi == 0), stop=(fi == 7))
                    return ins
                P.op("pe", mmo, r=["wo"] + [("mT", fi) for fi in range(8)], w=[("ps", 6), ("ps", 7)])
                self.ln_epilogue(slots[i], v, 2, [6, 7], i % 2)
                P.dma("sp", x2[t * 128:(t + 1) * 128, :], self.XB[:, slots[i], :], r=[("XB", slots[i])], w=[("x2", t)], is_out=True)


def _din(nc, name, shape, dt=F32):
    return nc.dram_tensor(name, list(shape), dt, kind="ExternalInput").ap()


def _dout(nc, name, shape, dt=F32):
    return nc.dram_tensor(name, list(shape), dt, kind="ExternalOutput").ap()


FFN_BLOCKS = [list(range(0, 6)), list(range(6, 12)), list(range(12, 18))]


def _s5_inputs(nc, sfx=""):
    return dict(s5lam=_din(nc, "s5lam" + sfx, [128, 3, 32]), s5b=_din(nc, "s5b" + sfx, [128, 2, 32, 16]),
                s5c=_din(nc, "s5c" + sfx, [128, 2, 32, 16]), s5exp=_din(nc, "s5exp" + sfx, [128, NEXP]),
                s5d=_din(nc, "s5d" + sfx, [128, 32]), maskLU=_din(nc, "maskLU" + sfx, [128, 2, 128]))


def _decl_A(nc, sfx=""):
    d = dict(cvec=_din(nc, "cvec" + sfx, [128, 8, 2]), wmod=_din(nc, "wmod" + sfx, [D, NMOD * D]),
             bmodT=_din(nc, "bmodT" + sfx, [1, NMOD * D]), lng=_din(nc, "lng" + sfx, [3, D]), lnb=_din(nc, "lnb" + sfx, [3, D]),
             wg=_din(nc, "wg" + sfx, [D, DFF]), wu=_din(nc, "wu" + sfx, [D, DFF]), wd=_din(nc, "wd" + sfx, [DFF, D]),
             win=_din(nc, "win" + sfx, [D, DIN]))
    d.update(_s5_inputs(nc, sfx))
    return d


def _emit_A(C, P, d, xa, x1, modscr, eout, stout, utout, tabout, xkey_in=None):
    C.mod_phase(d["cvec"], d["wmod"], None, modscr, bmod_row=d["bmodT"])
    P.barrier()
    C.load_mod(modscr, (0, 1, 2))
    C.load_ln(d["lng"], d["lnb"], 0)
    C.ffn_phase(d["wg"], d["wu"], d["wd"], xa, x1, FFN_BLOCKS, xkey_in=xkey_in, xkey_out="x1")
    P.barrier()
    C.load_mod(modscr, (3, 4, 5))
    C.s5_phase(x1, d["win"], d["s5lam"], d["s5b"], d["s5c"], d["s5exp"], d["s5d"], d["maskLU"], None, "A", eout=eout,
               stio=stout, utio=utout, tabio=tabout)


def _decl_B(nc):
    d = dict(xh=_din(nc, "xh", [512, D]), modscr=_din(nc, "modscr", [72, 2, 128]), lng=_din(nc, "lng", [3, D]),
             lnb=_din(nc, "lnb", [3, D]), win=_din(nc, "win", [D, DIN]), wglu=_din(nc, "wglu", [512, 512]),
             elist=_din(nc, "elist", [128, 3, 2, 32]), aflag=_din(nc, "aflag", [128, 3]),
             biasT=_din(nc, "biasT", [4, 128, 2, 7, 128]), maskT=_din(nc, "maskT", [128, 27, 128]),
             cw=_din(nc, "cw", [128, 4, 3]), cflag=_din(nc, "cflag", [128, 2]), wbranch=_din(nc, "wbranch", [3, 512, D]),
             wout=_din(nc, "wout", [D, D]), wg=_din(nc, "wg", [D, DFF]), wu=_din(nc, "wu", [D, DFF]), wd=_din(nc, "wd", [DFF, D]),
             stin=_din(nc, "stin", [2, 128, 9344]), utin=_din(nc, "utin", [128, 9216]), tabin=_din(nc, "tabin", [128, 6144]))
    d.update(_s5_inputs(nc))
    return d


def _emit_B(C, P, d, x1, x2, x3):
    modscr = d["modscr"]
    C.load_mod(modscr, (3, 4, 5))
    C.load_ln(d["lng"], d["lnb"], 1)
    C.s5_phase(x1, d["win"], d["s5lam"], d["s5b"], d["s5c"], d["s5exp"], d["s5d"], d["maskLU"], d["wglu"], "B",
               elist=d["elist"], aflag=d["aflag"], xkey=None, stio=d["stin"], utio=d["utin"], tabio=d["tabin"])
    hT = C.build_hT_full(x1, d["xh"], xkey=None)
    C.attn_phase(hT, d["win"], d["biasT"], d["maskT"])
    C.conv_phase(hT, d["win"], d["cw"], d["cflag"])
    C.merge_phase(x1, x2, d["win"], d["wbranch"], d["wout"], xkey=None)
    P.barrier()
    C.load_mod(modscr, (6, 7, 8))
    C.load_ln(d["lng"], d["lnb"], 2)
    C.ffn_phase(d["wg"], d["wu"], d["wd"], x2, x3, FFN_BLOCKS, xkey_in="x2", xkey_out="x3")


def build_A():
    nc = bass.Bass("TRN2", target_bir_lowering=False)
    xa = _din(nc, "xa", [NT * 128, D])
    idn = _din(nc, "idn", [128, 128])
    d = _decl_A(nc)
    x1 = _dout(nc, "x1", [NT * 128, D])
    modscr = _dout(nc, "modscr", [72, 2, 128])
    eout = _dout(nc, "eout", [2, 128, 2, 16])
    stout = _dout(nc, "stout", [2, 128, 9344])
    utout = _dout(nc, "utout", [128, 9216])
    tabout = _dout(nc, "tabout", [128, 6144])
    P = Prog(nc)
    C = Core(nc, P)
    C.load_consts(idn)
    _emit_A(C, P, d, xa, x1, modscr, eout, stout, utout, tabout)
    P.emit()
    return nc


def build_B(with_next=False, dbg=False):
    nc = bass.Bass("TRN2", target_bir_lowering=False)
    x1 = _din(nc, "x1", [NT * 128, D])
    idn = _din(nc, "idn", [128, 128])
    d = _decl_B(nc)
    x2 = nc.dram_tensor("x2", [NT * 128, D], F32).ap()
    if with_next:
        dn = _decl_A(nc, "_n")
        x3 = nc.dram_tensor("x3", [NT * 128, D], F32).ap()
        x1n = _dout(nc, "x1_n", [NT * 128, D])
        modscr_n = _dout(nc, "modscr_n", [72, 2, 128])
        eout_n = _dout(nc, "eout_n", [2, 128, 2, 16])
        stout_n = _dout(nc, "stout_n", [2, 128, 9344])
        utout_n = _dout(nc, "utout_n", [128, 9216])
        tabout_n = _dout(nc, "tabout_n", [128, 6144])
    else:
        x3 = _dout(nc, "x3", [NT * 128, D])
    P = Prog(nc)
    C = Core(nc, P)
    C.load_consts(idn)
    _emit_B(C, P, d, x1, x2, x3)
    if with_next:
        P.barrier()
        _emit_A(C, P, dn, x3, x1n, modscr_n, eout_n, stout_n, utout_n, tabout_n, xkey_in=None)
    P.emit()
    return nc


def _s5_host(inp, l):
    G, Pn, H = 32, 64, 16
    lam = np.stack([inp["s5_lam_re"][l], inp["s5_lam_im"][l],
                    np.broadcast_to(inp["s5_log_dt"][l][:, :, None], (2, G, Pn))], 0)
    s5lam = np.ascontiguousarray(lam.transpose(1, 3, 0, 2).reshape(128, 3, G))
    bb = np.stack([inp["s5_b_re"][l], inp["s5_b_im"][l]], 0)
    s5b = np.ascontiguousarray(bb.transpose(1, 3, 0, 2, 4).reshape(128, 2, G, H))
    cc = np.stack([inp["s5_c_re"][l], inp["s5_c_im"][l]], 0)
    s5c = np.ascontiguousarray(cc.transpose(1, 4, 0, 2, 3).reshape(128, 2, G, H))
    s = np.arange(8, dtype=np.float32)
    blk = 8.0 * (np.arange(16, dtype=np.float32) + 1.0)
    ef = np.concatenate([7 - s, s + 1, -(s + 1), [8.0], [2048.0], blk])
    eb = np.concatenate([s, 8 - s, s - 8, [8.0], [2048.0], blk])
    s5exp = np.concatenate([np.tile(ef, (64, 1)), np.tile(eb, (64, 1))], 0).astype(np.float32)
    d = inp["s5_d"][l].reshape(G, H)
    s5d = np.ascontiguousarray(np.tile(d.T[None], (8, 1, 1)).reshape(128, G))
    sidx = np.repeat(np.arange(8), 16)
    mL = (sidx[None, :] >= sidx[:, None]).astype(np.float32)
    mU = (sidx[None, :] <= sidx[:, None]).astype(np.float32)
    maskLU = np.ascontiguousarray(np.stack([mL, mU], 1))
    return dict(s5lam=s5lam.astype(np.float32), s5b=s5b.astype(np.float32), s5c=s5c.astype(np.float32), s5exp=s5exp,
                s5d=s5d.astype(np.float32), maskLU=maskLU)


def _bias_host(rpb):
    a = np.arange(2)[:, None, None, None]
    kc = np.arange(64)[None, :, None, None]
    b = np.arange(2)[None, None, :, None]
    qc = np.arange(64)[None, None, None, :]
    dc = np.clip(kc - qc + 15, 0, 30) + 0 * a + 0 * b
    out = np.zeros((4, 128, 2, 7, 128), np.float32)
    for ji, jo in enumerate(range(-1, 6)):
        dr = np.clip(2 * jo + a - b + 3, 0, 14) + 0 * kc + 0 * qc
        for h in range(8):
            out[h // 2, :, h % 2, ji, :] = rpb[h][dr, dc].reshape(128, 128)
    return out


def _mask_host(q):
    classes = [(0, list(range(0, 6))), (1, list(range(0, 5))), (2, list(range(0, 5))), (14, list(range(0, 5))), (15, list(range(-1, 5)))]
    out = np.zeros((128, 27, 128), np.float32)
    kc = np.arange(64)[:, None]
    qc = np.arange(64)[None, :]
    cs = np.clip(qc - 8, 0, 48)
    colv = (kc >= cs) & (kc < cs + 16)
    idx = 0
    for (m, jos) in classes:
        for jo in jos:
            for a in range(2):
                for b in range(2):
                    r = 32 * q + 2 * m + b
                    kr = 32 * q - 4 + 2 * (m + jo) + a
                    rs = min(max(r - 4, 0), 120)
                    ok = (rs <= kr < rs + 8)
                    v = colv if ok else np.zeros_like(colv)
                    out[a * 64:(a + 1) * 64, idx, b * 64:(b + 1) * 64] = np.where(v, 0.0, -30000.0)
            idx += 1
    return out


def _common_host(inp, l):
    d = _s5_host(inp, l)
    d["idn"] = np.eye(128, dtype=np.float32)
    d["lng"] = np.ascontiguousarray(inp["ln_g"][l])
    d["lnb"] = np.ascontiguousarray(inp["ln_b"][l])
    d["win"] = np.ascontiguousarray(inp["w_in"][l])
    return d


def _mapsA(inp, l, x, com, sfx=""):
    maps = []
    for c in range(8):
        b = c // 4
        cv = np.stack([inp["c"][b], inp["c_ctx"]], -1)
        m = {k + sfx: v for k, v in com.items() if k != "idn"}
        m.update({"cvec" + sfx: np.ascontiguousarray(cv.reshape(8, 128, 2).transpose(1, 0, 2)), "wmod" + sfx: inp["w_mod"][l],
                  "bmodT" + sfx: np.ascontiguousarray(inp["b_mod"][l].reshape(1, -1)),
                  "wg" + sfx: inp["ffn_wg"][l, 0], "wu" + sfx: inp["ffn_wu"][l, 0], "wd" + sfx: inp["ffn_wd"][l, 0]})
        if x is not None:
            m["xa"] = np.ascontiguousarray(x[c])
        maps.append(m)
    return maps


def _mapsB(inp, l, com, x1s, mods, eouts, sts, uts, tabs):
    biasT = _bias_host(inp["na_rpb"][l])
    cwh = np.ascontiguousarray(inp["conv_w"][l].reshape(3, 4, 128).transpose(2, 1, 0))
    maps = []
    for c in range(8):
        b, q = c // 4, c % 4
        E = [eouts[b * 4 + qq].transpose(1, 2, 0, 3).reshape(128, 2, 32) for qq in range(4)]
        xh = np.zeros((512, D), np.float32)
        if q > 0:
            xh[0:256] = x1s[c - 1][2048 - 256:2048]
        if q < 3:
            xh[256:512] = x1s[c + 1][0:256]
        el = np.zeros((128, 3, 2, 32), np.float32)
        af = np.zeros((128, 3), np.float32)
        for s in range(3):
            qf = q - 3 + s
            if qf >= 0:
                el[0:64, s] = E[qf][0:64]
                af[0:64, s] = 1.0
            qb = q + 3 - s
            if qb <= 3:
                el[64:128, s] = E[qb][64:128]
                af[64:128, s] = 1.0
        cfl = np.zeros((128, 2), np.float32)
        cfl[:, 0] = 1.0 if q > 0 else 0.0
        cfl[:, 1] = 1.0 if q < 3 else 0.0
        m = dict(com)
        m.update(x1=x1s[c], xh=xh, modscr=mods[c], stin=sts[c], utin=uts[c], tabin=tabs[c], wglu=np.ascontiguousarray(inp["s5_w_glu"][l]), elist=el, aflag=af,
                 biasT=biasT, maskT=_mask_host(q), cw=cwh, cflag=cfl, wbranch=np.ascontiguousarray(inp["w_branch"][l]),
                 wout=np.ascontiguousarray(inp["w_out"][l]), wg=inp["ffn_wg"][l, 1], wu=inp["ffn_wu"][l, 1], wd=inp["ffn_wd"][l, 1])
        maps.append(m)
    return maps


def kernel(**inp):
    inp = {k: np.asarray(v) for k, v in inp.items()}
    ncore = 8
    cores = list(range(ncore))
    x = [np.concatenate([inp["x"][c // 4, (c % 4) * 2048:(c % 4 + 1) * 2048], inp["ctx"][c // 4]], 0) for c in range(ncore)]
    com0 = _common_host(inp, 0)
    com1 = _common_host(inp, 1)
    m1 = _mapsA(inp, 0, x, com0)
    for m in m1:
        m["idn"] = com0["idn"]
    r1 = run_bass_kernel_spmd(build_A(), m1, core_ids=cores).results
    m2 = _mapsB(inp, 0, com0, [r["x1"] for r in r1], [r["modscr"] for r in r1], [r["eout"] for r in r1],
                [r["stout"] for r in r1], [r["utout"] for r in r1], [r["tabout"] for r in r1])
    mn = _mapsA(inp, 1, None, com1, "_n")
    for a, b_ in zip(m2, mn):
        a.update(b_)
    r2 = run_bass_kernel_spmd(build_B(with_next=True), m2, core_ids=cores).results
    m3 = _mapsB(inp, 1, com1, [r["x1_n"] for r in r2], [r["modscr_n"] for r in r2], [r["eout_n"] for r in r2],
                [r["stout_n"] for r in r2], [r["utout_n"] for r in r2], [r["tabout_n"] for r in r2])
    r3 = run_bass_kernel_spmd(build_B(with_next=False), m3, core_ids=cores).results
    out = np.zeros((2, 8192, D), np.float32)
    for c in range(ncore):
        out[c // 4, (c % 4) * 2048:(c % 4 + 1) * 2048] = r3[c]["x3"][0:2048]
    return out
```
